# Optimizing a Trainium2 kernel written in Bass

```python
import jax, jax.numpy as jnp
from jax import lax
import numpy as np

D_MODEL = 2048
BATCH = 1
SEQ = 8192
DEPTH = 4
DEC_BATCH = 8
DEC_SEQ = 64
PAST_LEN = 2048

CHUNK = 64
D_MIX = D_MODEL
D_A = D_MIX // 2
D_B = D_MIX - D_A
HGRN_DK = 128
H_A = D_A // HGRN_DK
HGRN_DV = D_A // H_A
H_B = 4
RET_DK = D_B // H_B
RET_DV = D_B // H_B
D_FF = 4 * D_MODEL
N_PROJ = 4 * D_A + 4 * D_B
N_MOD = 6
EPS = 1e-6
ROPE_BASE = 10000.0

kernel_name = "hymba_hgrn2_retention_stream_step"


def rms_norm(x, g):
    xf = x.astype(jnp.float32)
    y = xf * lax.rsqrt(jnp.mean(xf * xf, axis=-1, keepdims=True) + EPS)
    return y * g.astype(jnp.float32)


def to_chunks(t, c):
    b, l, h, d = t.shape
    return t.reshape(b, l // c, c, h, d).transpose(1, 0, 3, 2, 4)


def from_chunks(t):
    n, b, h, c, d = t.shape
    return t.transpose(1, 0, 3, 2, 4).reshape(b, n * c, h, d)


def rope(x, pos):
    half = x.shape[-1] // 2
    inv_freq = 1.0 / (ROPE_BASE ** jnp.linspace(0.0, 1.0, half, dtype=jnp.float32))
    ang = pos[:, None] * inv_freq[None, :]
    cos = jnp.cos(ang)[None, :, None, :]
    sin = jnp.sin(ang)[None, :, None, :]
    x1, x2 = x[..., :half], x[..., half:]
    return jnp.concatenate([x1 * cos - x2 * sin, x2 * cos + x1 * sin], axis=-1)


def hgrn2_chunkwise(q, k, v, log_f, s0):
    L = q.shape[1]
    c = min(CHUNK, L)
    mask = jnp.tril(jnp.ones((c, c), dtype=bool))
    xs = (to_chunks(q, c), to_chunks(k, c), to_chunks(v, c), to_chunks(log_f, c))

    def step(S, inp):
        qc, kc, vc, gc = inp
        b = jnp.cumsum(gc, axis=2)
        o_inter = jnp.einsum('bhtk,bhkv->bhtv', qc * jnp.exp(b), S)
        diff = b[:, :, :, None, :] - b[:, :, None, :, :]
        decay = jnp.exp(jnp.where(mask[None, None, :, :, None], diff, -jnp.inf))
        A = jnp.einsum('bhtk,bhsk,bhtsk->bhts', qc, kc, decay)
        o = o_inter + jnp.einsum('bhts,bhsv->bhtv', A, vc)
        b_last = b[:, :, -1:, :]
        S_new = jnp.exp(b_last[:, :, 0, :])[..., None] * S + jnp.einsum('bhsk,bhsv->bhkv', kc * jnp.exp(b_last - b), vc)
        return S_new, o

    s_final, o = lax.scan(step, s0.astype(jnp.float32), xs)
    return from_chunks(o), s_final


def retention_chunkwise(q, k, v, log_gamma, s0):
    L = q.shape[1]
    c = min(CHUNK, L)
    idx = jnp.arange(c, dtype=jnp.float32)
    lg = log_gamma[:, None, None]
    dmat = jnp.where(jnp.tril(jnp.ones((c, c), dtype=bool))[None],
                     jnp.exp((idx[:, None] - idx[None, :])[None] * lg), 0.0)
    xi = jnp.exp((idx + 1.0)[None, :] * log_gamma[:, None])[..., None]
    zeta = jnp.exp((c - 1.0 - idx)[None, :] * log_gamma[:, None])[..., None]
    g_c = jnp.exp(c * log_gamma)[:, None, None]
    xs = (to_chunks(q, c), to_chunks(k, c), to_chunks(v, c))

    def step(S, inp):
        qc, kc, vc = inp
        A = jnp.einsum('bhtk,bhsk->bhts', qc, kc) * dmat
        o = jnp.einsum('bhts,bhsv->bhtv', A, vc) + jnp.einsum('bhtk,bhkv->bhtv', qc, S) * xi
        S_new = g_c * S + jnp.einsum('bhsk,bhsv->bhkv', kc * zeta, vc)
        return S_new, o

    s_final, o = lax.scan(step, s0.astype(jnp.float32), xs)
    return from_chunks(o), s_final


def mixer(h, pos, s_a, s_b, lb, w_in, g_hn, g_rn, w_out, log_gamma):
    B, L, _ = h.shape
    proj = (h.astype(w_in.dtype) @ w_in).astype(jnp.float32)
    cuts = [D_A, 2 * D_A, 3 * D_A, 4 * D_A, 4 * D_A + D_B, 4 * D_A + 2 * D_B, 4 * D_A + 3 * D_B]
    qa, fa, ia, ga, qb, kb, vb, gb = jnp.split(proj, cuts, axis=-1)
    q_a = jax.nn.silu(qa.reshape(B, L, H_A, HGRN_DK))
    lb_h = lb.astype(jnp.float32).reshape(H_A, HGRN_DK)
    f = lb_h + (1.0 - lb_h) * jax.nn.sigmoid(fa.reshape(B, L, H_A, HGRN_DK))
    o_a, s_a_new = hgrn2_chunkwise(q_a, 1.0 - f, ia.reshape(B, L, H_A, HGRN_DV), jnp.log(f), s_a)
    o_a = rms_norm(o_a, g_hn.reshape(H_A, HGRN_DV)) * jax.nn.silu(ga.reshape(B, L, H_A, HGRN_DV))
    q_b = rope(qb.reshape(B, L, H_B, RET_DK), pos)
    k_b = rope(kb.reshape(B, L, H_B, RET_DK), pos) * (RET_DK ** -0.5)
    o_b, s_b_new = retention_chunkwise(q_b, k_b, vb.reshape(B, L, H_B, RET_DV), log_gamma, s_b)
    o_b = rms_norm(o_b, g_rn.reshape(H_B, RET_DV)) * jax.nn.silu(gb.reshape(B, L, H_B, RET_DV))
    o = jnp.concatenate([o_a.reshape(B, L, D_A), o_b.reshape(B, L, D_B)], axis=-1)
    return o.astype(w_out.dtype) @ w_out, s_a_new, s_b_new


def trunk(x, c, pos, s_a, s_b, lb_all, log_gamma, w_ada, b_ada, norm1_g, norm2_g, w_in,
          hgrn_norm_g, ret_norm_g, w_out, w_up, w_down, final_g):
    new_a, new_b = [], []
    c_act = jax.nn.silu(c.astype(jnp.float32))
    for l in range(DEPTH):
        mod = c_act @ w_ada[l].astype(jnp.float32) + b_ada[l].astype(jnp.float32)
        sh1, sc1, gt1, sh2, sc2, gt2 = [m[:, None, :] for m in jnp.split(mod, N_MOD, axis=-1)]
        h = rms_norm(x, norm1_g[l]) * (1.0 + sc1) + sh1
        o, sa, sb = mixer(h, pos, s_a[l], s_b[l], lb_all[l], w_in[l], hgrn_norm_g[l], ret_norm_g[l], w_out[l], log_gamma)
        x = (x.astype(jnp.float32) + gt1 * o.astype(jnp.float32)).astype(x.dtype)
        new_a.append(sa)
        new_b.append(sb)
        h = rms_norm(x, norm2_g[l]) * (1.0 + sc2) + sh2
        u = jnp.square(jax.nn.relu(h.astype(w_up.dtype) @ w_up[l]))
        y = u @ w_down[l]
        x = (x.astype(jnp.float32) + gt2 * y.astype(jnp.float32)).astype(x.dtype)
    y_out = rms_norm(x, final_g).astype(x.dtype)
    return y_out, jnp.stack(new_a, axis=0), jnp.stack(new_b, axis=0)


def setup_inputs(seed: int = 0) -> dict:
    key = jax.random.key(seed)
    ks = jax.random.split(key, 20)
    f32 = jnp.float32
    nrm = lambda k, shape, s: jax.random.normal(k, shape, f32) * s
    return {
        "x_prompt": nrm(ks[0], (BATCH, SEQ, D_MODEL), 1.0),
        "x_sample": nrm(ks[1], (DEC_BATCH, DEC_SEQ, D_MODEL), 1.0),
        "state_hgrn": nrm(ks[2], (DEPTH, DEC_BATCH, H_A, HGRN_DK, HGRN_DV), 0.5),
        "state_ret": nrm(ks[3], (DEPTH, DEC_BATCH, H_B, RET_DK, RET_DV), 0.5),
        "c_prompt": nrm(ks[4], (BATCH, D_MODEL), 1.0),
        "c_sample": nrm(ks[5], (DEC_BATCH, D_MODEL), 1.0),
        "lb_logits": nrm(ks[6], (DEPTH, D_A), 0.5),
        "w_ada": nrm(ks[7], (DEPTH, D_MODEL, N_MOD * D_MODEL), 0.5 * D_MODEL ** -0.5),
        "b_ada": nrm(ks[8], (DEPTH, N_MOD * D_MODEL), 0.02),
        "norm1_g": 1.0 + nrm(ks[9], (DEPTH, D_MODEL), 0.02),
        "norm2_g": 1.0 + nrm(ks[10], (DEPTH, D_MODEL), 0.02),
        "w_in": nrm(ks[11], (DEPTH, D_MODEL, N_PROJ), D_MODEL ** -0.5),
        "hgrn_norm_g": 1.0 + nrm(ks[12], (DEPTH, D_A), 0.02),
        "ret_norm_g": 1.0 + nrm(ks[13], (DEPTH, D_B), 0.02),
        "w_out": nrm(ks[14], (DEPTH, D_MIX, D_MODEL), D_MIX ** -0.5),
        "w_up": nrm(ks[15], (DEPTH, D_MODEL, D_FF), D_MODEL ** -0.5),
        "w_down": nrm(ks[16], (DEPTH, D_FF, D_MODEL), D_FF ** -0.5),
        "final_g": 1.0 + nrm(ks[17], (D_MODEL,), 0.02),
    }


def reference(x_prompt, x_sample, state_hgrn, state_ret, c_prompt, c_sample, lb_logits, w_ada, b_ada,
              norm1_g, norm2_g, w_in, hgrn_norm_g, ret_norm_g, w_out, w_up, w_down, final_g):
    p = jax.nn.softmax(lb_logits.astype(jnp.float32), axis=0)
    cs = jnp.cumsum(p, axis=0)
    lb_all = cs - cs[0:1]
    log_gamma = jnp.log1p(-jnp.exp2(-5.0 - jnp.arange(H_B, dtype=jnp.float32)))
    weights = (w_ada, b_ada, norm1_g, norm2_g, w_in, hgrn_norm_g, ret_norm_g, w_out, w_up, w_down, final_g)
    bp, lp = x_prompt.shape[0], x_prompt.shape[1]
    ls = x_sample.shape[1]
    pos_p = jnp.arange(lp, dtype=jnp.float32)
    pos_s = PAST_LEN + jnp.arange(ls, dtype=jnp.float32)
    zeros_a = jnp.zeros((DEPTH, bp, H_A, HGRN_DK, HGRN_DV), jnp.float32)
    zeros_b = jnp.zeros((DEPTH, bp, H_B, RET_DK, RET_DV), jnp.float32)
    y_prompt, sa_p, sb_p = trunk(x_prompt, c_prompt, pos_p, zeros_a, zeros_b, lb_all, log_gamma, *weights)
    y_sample, sa_s, sb_s = trunk(x_sample, c_sample, pos_s, state_hgrn, state_ret, lb_all, log_gamma, *weights)
    return (y_prompt, y_sample, sa_p, sb_p, sa_s, sb_s)
```

```python
import numpy as np
import ml_dtypes
from contextlib import ExitStack
import concourse.bass as bass
import concourse.mybir as mybir
from concourse.bass_utils import run_bass_kernel_spmd

F32 = mybir.dt.float32
BF16 = mybir.dt.bfloat16
U8 = mybir.dt.uint8
AF = mybir.ActivationFunctionType
ALU = mybir.AluOpType
AX = mybir.AxisListType

NCORE = 8
T = 1088
TP = 1024
NCH = 17
NPAIR = 9
L = 4
TBS = [(0, 512), (512, 512), (1024, 64)]
EPS = 1e-6
DBG_STOP = None
GAM = [1.0 - 2.0 ** (-5 - h) for h in range(4)]


class _Op:
    __slots__ = ("eng", "fn", "deps", "kind", "semkey", "idx", "sig", "cnt", "eidx", "inc")


class Prog:
    ENGS = ("pe", "act", "dve", "pool", "sp")

    def __init__(self):
        self.ops = []
        self.lastw = {}
        self.readers = {}

    def add(self, eng, fn, reads=(), writes=(), kind="c", semkey=None, inc=16):
        i = len(self.ops)
        deps = set()
        for r in reads:
            w = self.lastw.get(r)
            if w is not None:
                deps.add(w)
        for w_ in writes:
            w = self.lastw.get(w_)
            if w is not None:
                deps.add(w)
            for rd in self.readers.get(w_, {}).values():
                deps.add(rd)
        rk = ("dma", i) if kind == "dma" else eng
        for r in reads:
            self.readers.setdefault(r, {})[rk] = i
        for w_ in writes:
            self.lastw[w_] = i
            self.readers[w_] = {}
        op = _Op()
        op.eng, op.fn, op.deps, op.kind, op.semkey, op.idx = eng, fn, deps, kind, semkey, i
        op.sig, op.cnt, op.eidx, op.inc = False, 0, 0, inc
        self.ops.append(op)
        return i

    def dma(self, eng, fn, reads=(), writes=(), semkey=None, inc=16):
        return self.add(eng, fn, reads, writes, kind="dma", semkey=semkey, inc=inc)

    def emit(self, nc, stack):
        ops = self.ops
        nper = {e: 0 for e in self.ENGS}
        for op in ops:
            op.eidx = nper[op.eng]
            nper[op.eng] += 1
        waits = []
        for op in ops:
            wl = []
            for d in op.deps:
                dop = ops[d]
                if dop.kind == "dma":
                    wl.append(d)
                    continue
                if dop.eng == op.eng:
                    if op.eng == "pe":
                        continue
                    if op.kind != "dma" and (op.eidx - dop.eidx) > 2:
                        continue
                dop.sig = True
                wl.append(d)
            waits.append(wl)
        ecount = {e: 0 for e in self.ENGS}
        dcount = {}
        for op in ops:
            if op.kind == "dma":
                dcount[op.semkey] = dcount.get(op.semkey, 0) + op.inc
                op.cnt = dcount[op.semkey]
            elif op.sig:
                ecount[op.eng] += 1
                op.cnt = ecount[op.eng]
        print("signal counts", ecount, "dma", {k: v for k, v in dcount.items() if v > 256}, "nops", nper)
        esem = {e: stack.enter_context(nc.semaphore("s_" + e)) for e in self.ENGS}
        dsem = {k: stack.enter_context(nc.semaphore("d_%s" % (k,))) for k in dcount}
        per_eng = {e: [] for e in self.ENGS}
        for op in ops:
            per_eng[op.eng].append(op)

        def run(engname, eng):
            waited = {}
            for op in per_eng[engname]:
                need = {}
                for d in waits[op.idx]:
                    dop = ops[d]
                    if dop.kind == "dma":
                        key = ("d", dop.semkey)
                        sem = dsem[dop.semkey]
                    else:
                        key = ("e", dop.eng)
                        sem = esem[dop.eng]
                    if dop.cnt > need.get(key, (None, 0))[1]:
                        need[key] = (sem, dop.cnt)
                for key, (sem, cnt) in need.items():
                    if waited.get(key, 0) >= cnt:
                        continue
                    eng.wait_ge(sem, cnt)
                    waited[key] = cnt
                if op.fn is None:
                    continue
                ins = op.fn(eng)
                if op.kind == "dma":
                    ins.then_inc(dsem[op.semkey], op.inc)
                elif op.sig:
                    ins.then_inc(esem[op.eng], 1)

        block = stack.enter_context(nc.Block())

        @block.sync
        def _(e):
            run("sp", e)

        @block.tensor
        def _(e):
            run("pe", e)

        @block.scalar
        def _(e):
            run("act", e)

        @block.vector
        def _(e):
            run("dve", e)

        @block.gpsimd
        def _(e):
            run("pool", e)


def _need(name):
    if DBG_STOP is None:
        return True
    if DBG_STOP == 9:
        return False
    if name in ("w_in", "w_out"):
        return DBG_STOP >= 4
    return DBG_STOP >= 8


def build_program():
    nc = bass.Bass("TRN2", target_bir_lowering=False)
    P = Prog()

    def din(name, shape, dt=F32):
        return nc.dram_tensor(name, list(shape), dt, kind="ExternalInput").ap()

    def dout(name, shape, dt=F32):
        return nc.dram_tensor(name, list(shape), dt, kind="ExternalOutput").ap()

    xin = din("xin", [T, 2048])
    w_in = din("w_in", [L, 2048, 8192] if _need("w_in") else [1, 128, 256])
    w_out = din("w_out", [L, 2048, 2048] if _need("w_out") else [1, 128, 256])
    w_up = din("w_up", [L, 2048, 8192] if _need("w_up") else [1, 128, 256])
    w_down = din("w_down", [L, 8192, 2048] if _need("w_down") else [1, 128, 256])
    w_ada = din("w_ada", [L, 2048, 1536])
    cT_d = din("cT", [128, 16, 9])
    bT_d = din("bT", [128, L, 12])
    g1_d = din("g1T", [128, L, 16])
    g2_d = din("g2T", [128, L, 16])
    gf_d = din("gfT", [128, 16])
    ghn_d = din("ghnT", [128, L, 8])
    grn_d = din("grnT", [128, L, 8])
    lbl_d = din("lblT", [128, 8, L])
    cos_d = din("cosT", [128, T], BF16)
    sin_d = din("sinT", [128, T], BF16)
    mask_d = din("maskrep", [128, 4, 128], U8)
    xi_d = din("xi4", [128, 4, 64])
    gc_d = din("gc4", [128, 4, NCH])
    zinv_d = din("zinv", [128, 4])
    mA_d = din("maskA", [128, 8])
    cB_d = din("coefB", [128, 4, 8])
    oh_d = din("onehot", [128, 9])
    idb_d = din("identb", [128, 128], BF16)
    idf_d = din("identf", [128, 128])
    sta_d = din("st_a", [L, 8, 128, 128])
    stb_d = din("st_b", [L, 4, 256, 256])
    y_d = dout("y", [T, 2048])
    soa_d = dout("so_a", [L, 2, 8, 128, 128])
    sob_d = dout("so_b", [L, 2, 4, 256, 256])

    ag_mod_in = nc.dram_tensor("ag_mod_in", [128, 108 * L], F32)
    ag_mod_out = nc.dram_tensor("ag_mod_out", [1024, 108 * L], F32)
    agA_in = [[nc.dram_tensor("agA_in_%d_%d" % (l, h), [128, 129], F32) for h in range(8)] for l in range(L)]
    agA_out = [[nc.dram_tensor("agA_out_%d_%d" % (l, h), [1024, 129], F32) for h in range(8)] for l in range(L)]
    agB_in = [[nc.dram_tensor("agB_in_%d_%d" % (l, h), [128, 512], F32) for h in range(4)] for l in range(L)]
    agB_out = [[nc.dram_tensor("agB_out_%d_%d" % (l, h), [1024, 512], F32) for h in range(4)] for l in range(L)]

    st = ExitStack()

    def sb(name, shape, dt=F32):
        return st.enter_context(nc.sbuf_tensor("sb_" + name, list(shape), dt))

    xT = sb("xT", [128, 16, T])
    hT = sb("hT", [128, 16, T], BF16)
    wsl = sb("wsl", [128, 3, 16, 256], BF16)
    arena = sb("arena", [128, 18496], BF16)
    oTg = sb("oTg", [128, 4, T], BF16)
    K0 = sb("K0", [128, NPAIR, 256], BF16)
    K1 = sb("K1", [128, NPAIR, 256], BF16)
    ATb = sb("ATb", [128, NPAIR, 128], BF16)
    gat = sb("gat", [128, 8, 129])
    Sf = sb("Sf", [128, 512])
    SLs = sb("SLs", [128, 512])
    Sb = sb("Sb", [128, 512], BF16)
    Sinb = sb("Sinb", [128, 512], BF16)
    Sinsb = sb("Sinsb", [128, 512], BF16)
    GI = sb("GI", [128, 129])
    cosT = sb("cosT", [128, T], BF16)
    sinT = sb("sinT", [128, T], BF16)
    maskrep = sb("maskrep", [128, 4, 128], U8)
    xi4 = sb("xi4", [128, 4, 64])
    gc4 = sb("gc4", [128, 4, NCH])
    zinv = sb("zinv", [128, 4])
    maskA = sb("maskA", [128, 8])
    coefB = sb("coefB", [128, 4, 8])
    onehot = sb("onehot", [128, 9])
    identb = sb("identb", [128, 128], BF16)
    identf = sb("identf", [128, 128])
    onesb = sb("onesb", [128, 128], BF16)
    zcol = sb("zcol", [128, 1])
    cact = sb("cact", [128, 16, 9], BF16)
    bTs = sb("bTs", [128, L, 12])
    g1T = sb("g1T", [128, L, 16])
    g2T = sb("g2T", [128, L, 16])
    gfT = sb("gfT", [128, 16])
    ghnT = sb("ghnT", [128, L, 8])
    grnT = sb("grnT", [128, L, 8])
    lbl = sb("lbl", [128, 8, L])
    lbv = sb("lbv", [128, L, 8])
    omlv = sb("omlv", [128, L, 8])
    lbtmp = sb("lbtmp", [128, 8, 2])
    MOD = [sb("MOD%d" % s, [128, L, 96]) for s in range(2)]
    A1 = [sb("A1_%d" % s, [128, L, 16]) for s in range(2)]
    A2 = [sb("A2_%d" % s, [128, L, 16]) for s in range(2)]
    sm = sb("sm", [128, 4, NCH])
    dexp = sb("dexp", [128, 4, NCH])
    acoef = sb("acoef", [128, 8])
    epst = sb("epst", [128, 4])
    dts = sb("dts", [128, 2, 2])
    gatf = gat[:].rearrange("p a b -> p (a b)")
    SO = gatf[:, 0:512]
    stS = gatf[:, 512:1024]
    modp = gatf[:, 0:108 * L].rearrange("p (a b) -> p a b", b=9)
    cTs = gatf[:, 432:576].rearrange("p (a b) -> p a b", b=9)
    print("sbuf remaining", nc.sbuf_bytes_remaining)

    PS = [st.enter_context(nc.psum_tensor("ps%d" % b, [128, 512], F32)) for b in range(8)]
    PJ = [[PS[0], PS[1], PS[2]], [PS[3], PS[4], PS[5]]]
    M = [PS[6], PS[7]]
    Mb = [PS[6][:, :].bitcast(BF16), PS[7][:, :].bitcast(BF16)]

    def Fv(k):
        return arena[:, 2176 * k:2176 * (k + 1)].bitcast(F32)

    def Bv(k):
        return arena[:, 8704 + 1088 * k:8704 + 1088 * (k + 1)]

    uT = arena[:, 0:17408].rearrange("p (c t) -> p c t", t=T)

    def ukey(fc):
        return ("F", fc // 2) if fc < 8 else ("B", fc - 8)

    K0f = K0[:].rearrange("p a b -> p (a b)")[:, 0:2176].bitcast(F32)
    K1f = K1[:].rearrange("p a b -> p (a b)")[:, 0:2176].bitcast(F32)
    stage = arena[:, 0:4096].bitcast(F32)
    modall = arena[:, 0:1728 * L].bitcast(F32)

    def c3(ap):
        return ap.rearrange("p (c s) -> p c s", s=64)

    state = {"pjset": 0, "wslot": 0}

    def next_set():
        s = state["pjset"]
        state["pjset"] ^= 1
        return s

    def wload(src_ap, nkc):
        s = state["wslot"]
        state["wslot"] = (s + 1) % 3
        P.dma("pool", lambda e, s=s, src_ap=src_ap, nkc=nkc: e.dma_start(out=wsl[:, s, 0:nkc, :], in_=src_ap),
              writes=[("w", s)], semkey="w%d" % s)
        return s

    def wsrc(w, l, r0, nkc, c0):
        return w[l, r0:r0 + 128 * nkc, c0:c0 + 256].rearrange("(kc p) n -> p kc n", p=128)

    def proj(lhs_fn, nk, rhs_fn, reads, tbs=TBS):
        s = next_set()
        for kc in range(nk):
            for tb, (t0, tn) in enumerate(tbs):
                P.add("pe", lambda e, s=s, kc=kc, tb=tb, t0=t0, tn=tn: e.matmul(
                    PJ[s][tb][:, 0:tn], lhsT=lhs_fn(kc), rhs=rhs_fn(kc, t0, tn), start=(kc == 0), stop=(kc == nk - 1)),
                    reads=reads, writes=[("pj", s, tb)])
        return s

    def evac_act(s, dst_fn, func, writes, scale=1.0, extra_reads=()):
        for tb, (t0, tn) in enumerate(TBS):
            P.add("act", lambda e, s=s, tb=tb, t0=t0, tn=tn: e.activation(
                out=dst_fn(t0, tn), in_=PJ[s][tb][:, 0:tn], func=func, scale=scale),
                reads=[("pj", s, tb)] + list(extra_reads), writes=writes)

    def hrhs(kc, t0, tn):
        return hT[:, kc, t0:t0 + tn]

    cnt = [0]

    def ld(dst, src, key):
        cnt[0] += 1
        P.dma("sp", lambda e: e.dma_start(out=dst, in_=src), writes=[key], semkey="c%d" % cnt[0])

    for dst, src, key in [
        (cTs, cT_d, "gatA"), (bTs[:], bT_d, "bTs"), (g1T[:], g1_d, "g1T"), (g2T[:], g2_d, "g2T"), (gfT[:], gf_d, "gfT"),
        (ghnT[:], ghn_d, "ghnT"), (grnT[:], grn_d, "grnT"), (lbl[:], lbl_d, "lbl"), (cosT[:], cos_d, "cos"),
        (sinT[:], sin_d, "sin"), (maskrep[:], mask_d, "maskrep"), (xi4[:], xi_d, "xi4"), (gc4[:], gc_d, "gc4"),
        (zinv[:], zinv_d, "zinv"), (maskA[:], mA_d, "maskA"), (coefB[:], cB_d, "coefB"), (onehot[:], oh_d, "onehot"),
        (identb[:], idb_d, "identb"), (identf[:], idf_d, "identf"),
    ]:
        ld(dst, src, key)
    P.add("pool", lambda e: e.memset(onesb[:], 1.0), writes=["onesb"])
    P.add("pool", lambda e: e.memset(zcol[:], 0.0), writes=["zcol"])
    P.add("pool", lambda e: e.memset(epst[:, 0:1], 2048.0 * EPS), writes=["epst"])
    P.add("pool", lambda e: e.memset(epst[:, 1:2], 128.0 * EPS), writes=["epst"])
    P.add("pool", lambda e: e.memset(epst[:, 2:3], 256.0 * EPS), writes=["epst"])
    P.add("pool", lambda e: e.memset(ATb[:], 0.0), writes=["ATb"])

    P.add("dve", lambda e: e.tensor_reduce(out=lbtmp[:, :, 0], in_=lbl[:], axis=AX.X, op=ALU.max), reads=["lbl"], writes=["lbt0"])
    P.add("dve", lambda e: e.tensor_tensor(out=lbl[:], in0=lbl[:], in1=lbtmp[:, :, 0:1].to_broadcast([128, 8, L]), op=ALU.subtract),
          reads=["lbl", "lbt0"], writes=["lbl"])
    P.add("act", lambda e: e.activation(out=lbl[:], in_=lbl[:], func=AF.Exp), reads=["lbl"], writes=["lbl"])
    P.add("dve", lambda e: e.tensor_reduce(out=lbtmp[:, :, 1], in_=lbl[:], axis=AX.X, op=ALU.add), reads=["lbl"], writes=["lbt1"])
    P.add("dve", lambda e: e.reciprocal(out=lbtmp[:, :, 1], in_=lbtmp[:, :, 1]), reads=["lbt1"], writes=["lbt1"])
    P.add("dve", lambda e: e.tensor_tensor(out=lbl[:], in0=lbl[:], in1=lbtmp[:, :, 1:2].to_broadcast([128, 8, L]), op=ALU.mult),
          reads=["lbl", "lbt1"], writes=["lbl"])
    P.add("dve", lambda e: e.memset(lbv[:, 0, :], 0.0), reads=[], writes=["lbv"])
    for l in range(1, L):
        P.add("dve", lambda e, l=l: e.tensor_tensor(out=lbv[:, l, :], in0=lbv[:, l - 1, :], in1=lbl[:, :, l], op=ALU.add),
              reads=["lbl", "lbv"], writes=["lbv"])
    P.add("dve", lambda e: e.tensor_scalar(out=omlv[:], in0=lbv[:], scalar1=-1.0, scalar2=1.0, op0=ALU.mult, op1=ALU.add),
          reads=["lbv"], writes=["omlv"])

    P.add("act", lambda e: e.activation(out=cact[:], in_=cTs, func=AF.Silu), reads=["gatA", "gatB"], writes=["cact"])
    for l in range(L):
        for jb in range(6):
            s = wload(wsrc(w_ada, l, 0, 16, 256 * jb), 16)
            for cc in range(2):
                j = 2 * jb + cc
                ps = next_set()
                for kc in range(16):
                    P.add("pe", lambda e, s=s, kc=kc, cc=cc, ps=ps: e.matmul(
                        PJ[ps][0][:, 0:9], lhsT=wsl[:, s, kc, cc * 128:(cc + 1) * 128], rhs=cact[:, kc, :],
                        start=(kc == 0), stop=(kc == 15)), reads=[("w", s), "cact"], writes=[("pj", ps, 0)])
                P.add("act", lambda e, l=l, j=j, ps=ps: e.activation(
                    out=modp[:, l * 12 + j, :], in_=PJ[ps][0][:, 0:9], func=AF.Identity, bias=bTs[:, l, j:j + 1]),
                    reads=[("pj", ps, 0), "bTs"], writes=["gatA", "gatB"])
    P.dma("sp", lambda e: e.dma_start(out=ag_mod_in.ap(), in_=gatf[:, 0:108 * L]),
          reads=["gatA", "gatB"], writes=["ag_mod_in"], semkey="agm_st")
    P.dma("pool", lambda e: e.collective_compute("AllGather", ALU.bypass, replica_groups=[list(range(NCORE))],
                                                  ins=[ag_mod_in.ap()], outs=[ag_mod_out.ap()]),
          reads=["ag_mod_in"], writes=["ag_mod_out"], semkey="cc", inc=1)
    FKEYS = [("F", 0), ("F", 1), ("F", 2), ("F", 3)]
    P.dma("sp", lambda e: e.dma_start(out=modall.rearrange("p (r f) -> p r f", r=8),
                                      in_=ag_mod_out.ap().rearrange("(r p) f -> p r f", p=128)),
          reads=["ag_mod_out"], writes=FKEYS, semkey="agm_ld")
    ma5 = modall.rearrange("p (r l j w) -> p r l j w", r=8, l=L, j=12)
    for r in range(8):
        P.add("dve", lambda e, r=r: e.tensor_copy(out=MOD[0][:, :, 12 * r:12 * r + 12], in_=ma5[:, r, :, :, 0]),
              reads=FKEYS, writes=["MOD0"])
    ma3 = modall.rearrange("p (a w) -> p a w", w=9)
    P.add("dve", lambda e: e.tensor_tensor(out=ma3, in0=ma3, in1=onehot[:].unsqueeze(1).to_broadcast([128, 96 * L, 9]), op=ALU.mult),
          reads=FKEYS + ["onehot", "MOD0"], writes=FKEYS)
    msel = Bv(0)[:, 0:192 * L].bitcast(F32)
    P.add("dve", lambda e: e.tensor_reduce(out=msel, in_=ma3, axis=AX.X, op=ALU.add), reads=FKEYS, writes=[("B", 0)])
    ms4 = msel.rearrange("p (r l j) -> p r l j", r=8, l=L)
    for r in range(8):
        P.add("dve", lambda e, r=r: e.tensor_copy(out=MOD[1][:, :, 12 * r:12 * r + 12], in_=ms4[:, r, :, :]),
              reads=[("B", 0)], writes=["MOD1"])
    for sg in range(2):
        P.add("dve", lambda e, sg=sg: e.scalar_tensor_tensor(out=A1[sg][:], in0=MOD[sg][:, :, 16:32], scalar=1.0, in1=g1T[:],
                                                               op0=ALU.add, op1=ALU.mult),
              reads=["MOD%d" % sg, "g1T"], writes=["A1_%d" % sg])
        P.add("dve", lambda e, sg=sg: e.scalar_tensor_tensor(out=A2[sg][:], in0=MOD[sg][:, :, 64:80], scalar=1.0, in1=g2T[:],
                                                               op0=ALU.add, op1=ALU.mult),
              reads=["MOD%d" % sg, "g2T"], writes=["A2_%d" % sg])
    MODK = ["MOD0", "MOD1"]

    XK = [("x", fc) for fc in range(16)]
    for ti in range(NPAIR):
        n = 128 if ti < 8 else 64
        P.dma("sp", lambda e, ti=ti, n=n: e.dma_start(out=stage[0:n, :], in_=xin[128 * ti:128 * ti + n, :]),
              writes=[("F", 0), ("F", 1)], semkey="xld")
        for g in range(4):
            s = next_set()
            for q in range(4):
                fc = 4 * g + q
                P.add("pe", lambda e, s=s, q=q, fc=fc, n=n: e.transpose(
                    out=PJ[s][0][:, q * 128:q * 128 + n], in_=stage[0:n, fc * 128:(fc + 1) * 128], identity=identf[0:n, 0:n]),
                    reads=[("F", 0), ("F", 1), "identf"], writes=[("pj", s, 0)])
            P.add("act", lambda e, s=s, g=g, ti=ti, n=n: e.activation(
                out=xT[:, 4 * g:4 * g + 4, 128 * ti:128 * ti + n],
                in_=PJ[s][0][:, :].rearrange("p (q t) -> p q t", q=4)[:, :, 0:n], func=AF.Copy),
                reads=[("pj", s, 0)], writes=[("x", 4 * g + q) for q in range(4)])

    def norm(l, Acoef, Boff, final=False):
        s = next_set()
        for fc in range(16):
            sq = Bv(fc % 2)
            P.add("act", lambda e, fc=fc, sq=sq: e.activation(out=sq, in_=xT[:, fc, :], func=AF.Square),
                  reads=[("x", fc)], writes=[("B", fc % 2)])
            for tb, (t0, tn) in enumerate(TBS):
                P.add("pe", lambda e, fc=fc, sq=sq, tb=tb, t0=t0, tn=tn, s=s: e.matmul(
                    PJ[s][tb][:, 0:tn], lhsT=onesb[:], rhs=sq[:, t0:t0 + tn], start=(fc == 0), stop=(fc == 15)),
                    reads=[("B", fc % 2), "onesb"], writes=[("pj", s, tb)])
        rstd = Fv(3)
        for tb, (t0, tn) in enumerate(TBS):
            P.add("act", lambda e, tb=tb, t0=t0, tn=tn, s=s: e.activation(
                out=rstd[:, t0:t0 + tn], in_=PJ[s][tb][:, 0:tn], func=AF.Sqrt, bias=epst[:, 0:1]),
                reads=[("pj", s, tb), "epst"], writes=[("F", 3)])
        P.add("dve", lambda e: e.reciprocal(out=rstd, in_=rstd), reads=[("F", 3)], writes=[("F", 3)])
        rt = 2048.0 ** 0.5
        for fc in range(16):
            tmp = Fv(fc % 2 + 1)
            if final:
                P.add("dve", lambda e, fc=fc, tmp=tmp: e.scalar_tensor_tensor(
                    out=tmp, in0=xT[:, fc, :], scalar=gfT[:, fc:fc + 1], in1=rstd, op0=ALU.mult, op1=ALU.mult),
                    reads=[("x", fc), ("F", 3), "gfT"], writes=[("F", fc % 2 + 1)])
                P.add("act", lambda e, fc=fc, tmp=tmp: e.activation(out=xT[:, fc, :], in_=tmp, func=AF.Copy, scale=rt),
                      reads=[("F", fc % 2 + 1)], writes=[("x", fc)])
                continue
            for sg, (t0, tn) in enumerate([(0, TP), (TP, 64)]):
                P.add("dve", lambda e, fc=fc, tmp=tmp, sg=sg, t0=t0, tn=tn: e.scalar_tensor_tensor(
                    out=tmp[:, t0:t0 + tn], in0=xT[:, fc, t0:t0 + tn], scalar=Acoef[sg][:, l, fc:fc + 1], in1=rstd[:, t0:t0 + tn],
                    op0=ALU.mult, op1=ALU.mult),
                    reads=[("x", fc), ("F", 3), "A1_0", "A1_1", "A2_0", "A2_1"], writes=[("F", fc % 2 + 1)])
                P.add("act", lambda e, fc=fc, tmp=tmp, sg=sg, t0=t0, tn=tn: e.activation(
                    out=hT[:, fc, t0:t0 + tn], in_=tmp[:, t0:t0 + tn], func=AF.Identity, scale=rt,
                    bias=MOD[sg][:, l, Boff + fc:Boff + fc + 1]),
                    reads=[("F", fc % 2 + 1)] + MODK, writes=["h"])

    def transposes(src_list, dstK, dkey, col0s, scale_ap=None, skeys=()):
        for src, col0, skey in zip(src_list, col0s, skeys):
            for (m, p0, np_) in [(0, 0, 8), (1, 8, 1)]:
                for p in range(p0, p0 + np_):
                    n = 128 if p < 8 else 64
                    P.add("pe", lambda e, src=src, p=p, n=n, m=m, p0=p0: e.transpose(
                        out=Mb[m][0:n, (p - p0) * 128:(p - p0) * 128 + 128], in_=src[:, 128 * p:128 * p + n], identity=identb[:]),
                        reads=[skey, "identb"], writes=[("m", m)])
                n = 128 if p0 == 0 else 64
                src_ps = Mb[m][0:n, 0:np_ * 128].rearrange("p (a b) -> p a b", b=128)
                dst = dstK[0:n, p0:p0 + np_, col0:col0 + 128]
                if scale_ap is None:
                    P.add("act", lambda e, dst=dst, src_ps=src_ps: e.activation(out=dst, in_=src_ps, func=AF.Copy),
                          reads=[("m", m)], writes=[dkey])
                else:
                    P.add("act", lambda e, dst=dst, src_ps=src_ps, n=n: e.activation(
                        out=dst, in_=src_ps, func=AF.Identity, scale=scale_ap[0:n, :]),
                        reads=[("m", m), "zinv"], writes=[dkey])

    def build_AT(k_list, q_list, kkeys, qkeys):
        nk = len(k_list)
        for g, (p0, np_) in enumerate([(0, 4), (4, 4), (8, 1)]):
            m = g % 2
            for p in range(p0, p0 + np_):
                n = 128 if p < 8 else 64
                for kc in range(nk):
                    P.add("pe", lambda e, p=p, n=n, kc=kc, m=m, p0=p0: e.matmul(
                        M[m][0:n, (p - p0) * 128:(p - p0) * 128 + n], lhsT=k_list[kc][:, 128 * p:128 * p + n],
                        rhs=q_list[kc][:, 128 * p:128 * p + n], start=(p == p0 and kc == 0), stop=(kc == nk - 1),
                        skip_group_check=True),
                        reads=list(kkeys) + list(qkeys), writes=[("m", m)])
            n = 128 if p0 < 8 else 64
            P.add("dve", lambda e, m=m, p0=p0, np_=np_, n=n: e.copy_predicated(
                out=ATb[0:n, p0:p0 + np_, 0:n], mask=maskrep[0:n, 0:np_, 0:n],
                data=M[m][0:n, 0:np_ * 128].rearrange("p (a b) -> p a b", b=128)[:, :, 0:n]),
                reads=[("m", m), "maskrep", "ATb"], writes=["ATb"])

    def sumsq_norm(ol_list, olkeys, gains, gates, gkeys, fcs, N):
        nv = len(ol_list)
        for vc in range(nv):
            P.add("act", lambda e, vc=vc: e.activation(out=Bv(vc), in_=ol_list[vc], func=AF.Square),
                  reads=[olkeys[vc]], writes=[("B", vc)])
        rstd = Fv(0)
        for tb, (t0, tn) in enumerate(TBS):
            m = tb % 2
            for vc in range(nv):
                P.add("pe", lambda e, vc=vc, t0=t0, tn=tn, m=m: e.matmul(
                    M[m][:, 0:tn], lhsT=onesb[:], rhs=Bv(vc)[:, t0:t0 + tn], start=(vc == 0), stop=(vc == nv - 1)),
                    reads=[("B", vc), "onesb"], writes=[("m", m)])
            P.add("act", lambda e, t0=t0, tn=tn, m=m: e.activation(
                out=rstd[:, t0:t0 + tn], in_=M[m][:, 0:tn], func=AF.Sqrt, bias=epst[:, (1 if N == 128 else 2):(2 if N == 128 else 3)]),
                reads=[("m", m), "epst"], writes=[("F", 0)])
        P.add("dve", lambda e: e.reciprocal(out=rstd, in_=rstd), reads=[("F", 0)], writes=[("F", 0)])
        rt = float(N) ** 0.5
        for vc in range(nv):
            P.add("dve", lambda e, vc=vc: e.scalar_tensor_tensor(
                out=ol_list[vc], in0=ol_list[vc], scalar=rt, in1=rstd, op0=ALU.mult, op1=ALU.mult),
                reads=[olkeys[vc], ("F", 0)], writes=[olkeys[vc]])
            P.add("dve", lambda e, vc=vc: e.scalar_tensor_tensor(
                out=oTg[:, fcs[vc] % 4, :], in0=ol_list[vc], scalar=gains[vc], in1=gates[vc], op0=ALU.mult, op1=ALU.mult),
                reads=[olkeys[vc], gkeys[vc], "ghnT", "grnT"], writes=[("og", fcs[vc] % 4)])

    oc = [0]

    def out_dma(dst, src, reads):
        oc[0] += 1
        P.dma("sp", lambda e: e.dma_start(out=dst, in_=src), reads=reads, writes=[("out", oc[0])], semkey="o%d" % (oc[0] % 8))

    def hgrn_load(l, hd):
        c0 = 512 * hd
        return (wload(wsrc(w_in, l, 0, 16, c0), 16), wload(wsrc(w_in, l, 0, 16, c0 + 256), 16))

    def hgrn_head(l, hd, part, slots=None):
        par = hd % 2
        F0, F1, F2, F3 = Fv(0), Fv(1), Fv(2), Fv(3)
        B0, B1, B2, B4, B5 = Bv(0), Bv(1), Bv(2), Bv(4), Bv(5)
        B3, K3 = (Bv(3), ("B", 3)) if par == 0 else (Bv(7), ("B", 7))
        B6, K6 = (Bv(6), ("B", 6)) if par == 0 else (Bv(8), ("B", 8))
        if part == "front":
            hgrn_front(l, hd, slots, F0, F1, F2, F3, B0, B1, B2, B3, K3, B4, B5, B6, K6)
        elif part == "chain":
            hgrn_chain(l, hd, par, F3, B2)
        else:
            hgrn_tail(l, hd, par, F0, F1, F3, B3, K3, B6, K6)

    def hgrn_front(l, hd, slots, F0, F1, F2, F3, B0, B1, B2, B3, K3, B4, B5, B6, K6):
        s0, s1 = slots
        w0 = lambda c: (lambda kc: wsl[:, s0, kc, c * 128:(c + 1) * 128])
        w1 = lambda c: (lambda kc: wsl[:, s1, kc, c * 128:(c + 1) * 128])
        ps = proj(w0(1), 16, hrhs, [("w", s0), "h"])
        evac_act(ps, lambda t0, tn: F0[:, t0:t0 + tn], AF.Sigmoid, [("F", 0)])
        P.add("dve", lambda e: e.tensor_scalar(out=F0, in0=F0, scalar1=omlv[:, l, hd:hd + 1], scalar2=lbv[:, l, hd:hd + 1],
                                               op0=ALU.mult, op1=ALU.add), reads=[("F", 0), "omlv", "lbv"], writes=[("F", 0)])
        P.add("act", lambda e: e.activation(out=F1, in_=F0, func=AF.Ln), reads=[("F", 0)], writes=[("F", 1)])
        P.add("dve", lambda e: e.tensor_scalar(out=B0, in0=F0, scalar1=-1.0, scalar2=1.0, op0=ALU.mult, op1=ALU.add),
              reads=[("F", 0)], writes=[("B", 0)])
        for (t0, tn) in [(0, TP), (TP, 64)]:
            P.add("dve", lambda e, t0=t0, tn=tn: e.tensor_tensor_scan(
                out=F2[:, t0:t0 + tn], data0=F1[:, t0:t0 + tn], data1=zcol[:].to_broadcast([128, tn]), initial=0.0,
                op0=ALU.add, op1=ALU.add), reads=[("F", 1), "zcol"], writes=[("F", 2)])
        F2c = c3(F2)
        P.add("dve", lambda e: e.tensor_tensor(out=c3(F1), in0=F2c, in1=F2c[:, :, 31:32].to_broadcast([128, NCH, 64]), op=ALU.subtract),
              reads=[("F", 2)], writes=[("F", 1)])
        P.add("dve", lambda e: e.tensor_copy(out=sm[:, 3, :], in_=F2c[:, :, 63]), reads=[("F", 2)], writes=["sm"])
        P.add("dve", lambda e: e.tensor_tensor(out=sm[:, 0, 1:16], in0=F2c[:, 1:16, 31], in1=F2c[:, 0:15, 31], op=ALU.subtract),
              reads=[("F", 2), "sm"], writes=["sm"])
        P.add("dve", lambda e: e.tensor_tensor(out=sm[:, 1, :], in0=F2c[:, :, 63], in1=F2c[:, :, 31], op=ALU.subtract),
              reads=[("F", 2), "sm"], writes=["sm"])
        P.add("act", lambda e: e.activation(out=dexp[:], in_=sm[:], func=AF.Exp), reads=["sm"], writes=["dexp"])
        P.add("act", lambda e: e.activation(out=F0, in_=F1, func=AF.Exp), reads=[("F", 1)], writes=[("F", 0)])
        P.add("act", lambda e: e.activation(out=F1, in_=F1, func=AF.Exp, scale=-1.0), reads=[("F", 1)], writes=[("F", 1)])
        P.add("act", lambda e: e.activation(out=F2, in_=F2, func=AF.Exp), reads=[("F", 2)], writes=[("F", 2)])
        ps = proj(w0(0), 16, hrhs, [("w", s0), "h"])
        evac_act(ps, lambda t0, tn: B1[:, t0:t0 + tn], AF.Silu, [("B", 1)])
        P.add("dve", lambda e: e.tensor_tensor(out=B2, in0=B1, in1=F0, op=ALU.mult), reads=[("B", 1), ("F", 0)], writes=[("B", 2)])
        P.add("dve", lambda e: e.tensor_tensor(out=B3, in0=B1, in1=F2, op=ALU.mult), reads=[("B", 1), ("F", 2)], writes=[K3])
        P.add("dve", lambda e: e.tensor_tensor(out=B4, in0=B0, in1=F1, op=ALU.mult), reads=[("B", 0), ("F", 1)], writes=[("B", 4)])
        ps = proj(w1(0), 16, hrhs, [("w", s1), "h"])
        evac_act(ps, lambda t0, tn: B5[:, t0:t0 + tn], AF.Copy, [("B", 5)])
        ps = proj(w1(1), 16, hrhs, [("w", s1), "h"])
        evac_act(ps, lambda t0, tn: B6[:, t0:t0 + tn], AF.Silu, [K6])
        transposes([B5], K0, "K0", [0], skeys=[("B", 5)])
        transposes([B4], K1, "K1", [0], skeys=[("B", 4)])
        build_AT([B4], [B2], [("B", 4)], [("B", 2)])

    def hgrn_chain(l, hd, par, F3, B2):
        U = [Sf[:, 0:128], Sf[:, 128:256]]
        UK = ["U0", "U1"]
        for p in range(NPAIR):
            n = 128 if p < 8 else 64
            q4 = p % 4
            P.add("pe", lambda e, p=p, n=n, q4=q4: e.matmul(
                M[1][:, q4 * 128:q4 * 128 + n], lhsT=K0[0:n, p, 0:128], rhs=ATb[0:n, p, 0:n], start=(q4 == 0), stop=False,
                skip_group_check=True), reads=["K0", "ATb"], writes=[("m", 1)])
            if p % 2 == 0:
                for cidx, c in enumerate(range(2 * p, min(2 * p + 4, NCH))):
                    pp, half = c // 2, c % 2
                    pbank = M[0] if half == 0 else PJ[0][0]
                    pkey = ("m", 0) if half == 0 else ("pj", 0, 0)
                    P.add("pe", lambda e, cidx=cidx, pp=pp, half=half, pbank=pbank: e.matmul(
                        pbank[:, (cidx // 2) * 128:(cidx // 2 + 1) * 128], lhsT=K1[64 * half:64 * half + 64, pp, 0:128],
                        rhs=K0[64 * half:64 * half + 64, pp, 0:128], start=(cidx < 2), stop=True, skip_group_check=True),
                        reads=["K0", "K1"], writes=[pkey])
            for c in range(2 * p, min(2 * p + 2, NCH)):
                half = c % 2
                first = (c == 0 or c == 16)
                cidx = c % 4
                pbank = M[0] if half == 0 else PJ[0][0]
                pkey = ("m", 0) if half == 0 else ("pj", 0, 0)
                psl = pbank[:, (cidx // 2) * 128:(cidx // 2 + 1) * 128]
                cur, prv = c % 2, (c + 1) % 2
                if first:
                    P.add("dve", lambda e, psl=psl, cur=cur: e.tensor_copy(out=U[cur], in_=psl), reads=[pkey], writes=[UK[cur]])
                else:
                    P.add("act", lambda e, c=c, prv=prv: e.activation(out=Sb[:, 0:128], in_=U[prv], func=AF.Identity,
                                                                      scale=dexp[:, 0, c:c + 1]),
                          reads=[UK[prv], "dexp"], writes=["Sb"])
                    P.add("pe", lambda e, c=c, q4=q4, half=half: e.matmul(
                        M[1][:, q4 * 128 + 64 * half:q4 * 128 + 64 * half + 64], lhsT=Sb[:, 0:128], rhs=B2[:, 64 * c:64 * c + 64],
                        start=False, stop=True, skip_group_check=True), reads=["Sb", ("B", 2)], writes=[("m", 1)])
                    P.add("dve", lambda e, c=c, psl=psl, cur=cur, prv=prv: e.scalar_tensor_tensor(
                        out=U[cur], in0=U[prv], scalar=dexp[:, 0, c:c + 1], in1=psl, op0=ALU.mult, op1=ALU.add),
                        reads=[UK[prv], pkey, "dexp"], writes=[UK[cur]])
                if c == 15:
                    P.add("dve", lambda e, cur=cur: e.tensor_scalar(out=GI[:, 0:128], in0=U[cur], scalar1=dexp[:, 1, 15:16], scalar2=None,
                                                                    op0=ALU.mult), reads=[UK[cur], "dexp"], writes=["GI"])
                if c == 16:
                    P.add("dve", lambda e, cur=cur: e.tensor_scalar(out=SLs[:, 0:128], in0=U[cur], scalar1=dexp[:, 1, 16:17], scalar2=None,
                                                                    op0=ALU.mult), reads=[UK[cur], "dexp"], writes=["SLs"])
            if q4 == 3 or p == 8:
                g = p // 4
                ncol = 512 if p < 8 else 64
                P.add("act", lambda e, g=g, ncol=ncol: e.activation(out=F3[:, 512 * g:512 * g + ncol], in_=M[1][:, 0:ncol], func=AF.Copy),
                      reads=[("m", 1)], writes=[("F", 3)])
        P.add("act", lambda e: e.activation(out=GI[:, 128:129], in_=dexp[:, 3, 15:16], func=AF.Copy), reads=["dexp"], writes=["GI"])
        P.add("act", lambda e: e.activation(out=dts[:, par, :], in_=dexp[:, 3, 15:17], func=AF.Copy), reads=["dexp"], writes=[("dts", par)])
        P.dma("sp", lambda e: e.dma_start(out=agA_in[l][hd].ap(), in_=GI[:]), reads=["GI"], writes=[("agin", l, hd)], semkey="ag_st")
        P.dma("pool", lambda e: e.collective_compute("AllGather", ALU.bypass, replica_groups=[list(range(NCORE))],
                                                      ins=[agA_in[l][hd].ap()], outs=[agA_out[l][hd].ap()]),
              reads=[("agin", l, hd)], writes=[("agout", l, hd)], semkey="cc", inc=1)

    def hgrn_tail(l, hd, par, F0, F1, F3, B3, K3, B6, K6):
        P.dma("sp", lambda e: e.dma_start(out=gat[:], in_=agA_out[l][hd].ap().rearrange("(r p) f -> p r f", p=128)),
              reads=[("agout", l, hd)], writes=["gatA", "gatB"], semkey="ag_ld")
        P.add("dve", lambda e: e.scalar_tensor_tensor(out=acoef[:], in0=gat[:, :, 128], scalar=-1.0, in1=maskA[:],
                                                      op0=ALU.add, op1=ALU.mult), reads=["gatA", "gatB", "maskA"], writes=["acoef"])
        P.add("dve", lambda e: e.tensor_scalar(out=acoef[:], in0=acoef[:], scalar1=1.0, scalar2=None, op0=ALU.add),
              reads=["acoef"], writes=["acoef"])
        P.add("dve", lambda e: e.tensor_tensor(out=gat[:, :, 0:128], in0=gat[:, :, 0:128],
                                               in1=maskA[:].unsqueeze(2).to_broadcast([128, 8, 128]), op=ALU.mult),
              reads=["gatA", "gatB", "maskA"], writes=["gatA", "gatB"])
        Sin = F1[:, 0:128]
        P.add("dve", lambda e: e.tensor_copy(out=Sin, in_=gat[:, 0, 0:128]), reads=["gatA", "gatB"], writes=[("F", 1)])
        for j in range(1, 7):
            P.add("dve", lambda e, j=j: e.scalar_tensor_tensor(out=Sin, in0=Sin, scalar=acoef[:, j:j + 1], in1=gat[:, j, 0:128],
                                                               op0=ALU.mult, op1=ALU.add),
                  reads=["gatA", "gatB", "acoef", ("F", 1)], writes=[("F", 1)])
        P.add("act", lambda e: e.activation(out=Sinb[:, 0:128], in_=Sin, func=AF.Copy), reads=[("F", 1)], writes=["Sinb"])
        P.dma("sp", lambda e: e.dma_start(out=stS[:, 0:128], in_=sta_d[l, hd]), writes=["gatB"], semkey="stS")
        P.add("act", lambda e: e.activation(out=Sinsb[:, 0:128], in_=stS[:, 0:128], func=AF.Copy), reads=["gatB"], writes=["Sinsb"])
        P.add("dve", lambda e: e.scalar_tensor_tensor(out=SO[:, 0:128], in0=Sin, scalar=dts[:, par, 0:1], in1=GI[:, 0:128],
                                                      op0=ALU.mult, op1=ALU.add), reads=[("F", 1), ("dts", par), "GI"], writes=["gatA"])
        P.add("dve", lambda e: e.scalar_tensor_tensor(out=SO[:, 128:256], in0=stS[:, 0:128], scalar=dts[:, par, 1:2], in1=SLs[:, 0:128],
                                                      op0=ALU.mult, op1=ALU.add), reads=["gatB", ("dts", par), "SLs"], writes=["gatA"])
        oc[0] += 1
        P.dma("sp", lambda e: e.dma_start(out=soa_d[l, 0, hd], in_=SO[:, 0:128]), reads=["gatA"], writes=[("out", oc[0]), "gatA"], semkey="so")
        oc[0] += 1
        P.dma("sp", lambda e: e.dma_start(out=soa_d[l, 1, hd], in_=SO[:, 128:256]), reads=["gatA"], writes=[("out", oc[0]), "gatA"], semkey="so1")
        for tb, (t0, tn) in enumerate(TBS):
            m = tb % 2
            lhs = Sinb if tb < 2 else Sinsb
            P.add("pe", lambda e, t0=t0, tn=tn, m=m, lhs=lhs: e.matmul(M[m][:, 0:tn], lhsT=lhs[:, 0:128], rhs=B3[:, t0:t0 + tn],
                                                                     start=True, stop=True),
                  reads=["Sinb", "Sinsb", K3], writes=[("m", m)])
            P.add("dve", lambda e, t0=t0, tn=tn, m=m: e.tensor_tensor(out=F3[:, t0:t0 + tn], in0=M[m][:, 0:tn], in1=F3[:, t0:t0 + tn], op=ALU.add),
                  reads=[("m", m), ("F", 3)], writes=[("F", 3)])
        sumsq_norm([F3], [("F", 3)], [ghnT[:, l, hd:hd + 1]], [B6], [K6], [hd], 128)

    def ret_head(l, hb):
        c0 = 4096 + 1024 * hb
        F0, F1, F2, F3 = Fv(0), Fv(1), Fv(2), Fv(3)
        B = [Bv(k) for k in range(8)]
        g64 = GAM[hb] ** 64

        def rope(blk, o1, o2, k1, k2):
            s = wload(wsrc(w_in, l, 0, 16, c0 + 256 * blk), 16)
            for cc, dst, dk in [(0, B[0], ("B", 0)), (1, B[1], ("B", 1))]:
                ps = proj(lambda kc, cc=cc: wsl[:, s, kc, cc * 128:(cc + 1) * 128], 16, hrhs, [("w", s), "h"])
                evac_act(ps, lambda t0, tn, dst=dst: dst[:, t0:t0 + tn], AF.Copy, [dk])
            P.add("dve", lambda e: e.tensor_tensor(out=F0, in0=B[0], in1=cosT[:], op=ALU.mult), reads=[("B", 0), "cos"], writes=[("F", 0)])
            P.add("dve", lambda e: e.tensor_tensor(out=F1, in0=B[1], in1=sinT[:], op=ALU.mult), reads=[("B", 1), "sin"], writes=[("F", 1)])
            P.add("dve", lambda e: e.tensor_tensor(out=o1, in0=F0, in1=F1, op=ALU.subtract), reads=[("F", 0), ("F", 1)], writes=[k1])
            P.add("dve", lambda e: e.tensor_tensor(out=F0, in0=B[1], in1=cosT[:], op=ALU.mult), reads=[("B", 1), "cos"], writes=[("F", 0)])
            P.add("dve", lambda e: e.tensor_tensor(out=F1, in0=B[0], in1=sinT[:], op=ALU.mult), reads=[("B", 0), "sin"], writes=[("F", 1)])
            P.add("dve", lambda e: e.tensor_tensor(out=o2, in0=F0, in1=F1, op=ALU.add), reads=[("F", 0), ("F", 1)], writes=[k2])

        rope(0, B[2], B[3], ("B", 2), ("B", 3))
        rope(1, B[4], B[5], ("B", 4), ("B", 5))
        QR = [B[2], B[3]]
        KR = [B[4], B[5]]
        sv = wload(wsrc(w_in, l, 0, 16, c0 + 512), 16)
        for cc in range(2):
            ps = proj(lambda kc, cc=cc: wsl[:, sv, kc, cc * 128:(cc + 1) * 128], 16, hrhs, [("w", sv), "h"])
            evac_act(ps, lambda t0, tn, cc=cc: B[cc][:, t0:t0 + tn], AF.Copy, [("B", cc)])
        sg_ = wload(wsrc(w_in, l, 0, 16, c0 + 768), 16)
        for cc in range(2):
            ps = proj(lambda kc, cc=cc: wsl[:, sg_, kc, cc * 128:(cc + 1) * 128], 16, hrhs, [("w", sg_), "h"])
            evac_act(ps, lambda t0, tn, cc=cc: B[6 + cc][:, t0:t0 + tn], AF.Silu, [("B", 6 + cc)])
        transposes([B[0], B[1]], K0, "K0", [0, 128], scale_ap=zinv[:, hb:hb + 1], skeys=[("B", 0), ("B", 1)])
        transposes(KR, K1, "K1", [0, 128], skeys=[("B", 4), ("B", 5)])
        build_AT(KR, QR, [("B", 4), ("B", 5)], [("B", 2), ("B", 3)])
        OL = [F2, F3]
        UB = [Sf, SLs]
        UK = ["Sf", "SLs"]
        for p in range(NPAIR):
            n = 128 if p < 8 else 64
            q2 = p % 2
            for vc in range(2):
                P.add("pe", lambda e, p=p, n=n, q2=q2, vc=vc: e.matmul(
                    M[1][:, vc * 256 + q2 * 128:vc * 256 + q2 * 128 + n], lhsT=K0[0:n, p, vc * 128:(vc + 1) * 128], rhs=ATb[0:n, p, 0:n],
                    start=(q2 == 0 and vc == 0), stop=False, skip_group_check=True), reads=["K0", "ATb"], writes=[("m", 1)])
            for c in range(2 * p, min(2 * p + 2, NCH)):
                half = c % 2
                first = (c == 0 or c == 16)
                cur, prv = c % 2, (c + 1) % 2
                pbank = M[0] if half == 0 else PJ[0][0]
                pkey = ("m", 0) if half == 0 else ("pj", 0, 0)
                for kc in range(2):
                    P.add("pe", lambda e, p=p, half=half, kc=kc, pbank=pbank: e.matmul(
                        pbank[:, kc * 256:(kc + 1) * 256], lhsT=K1[64 * half:64 * half + 64, p, kc * 128:(kc + 1) * 128],
                        rhs=K0[64 * half:64 * half + 64, p, 0:256], start=(kc == 0), stop=True, skip_group_check=True),
                        reads=["K0", "K1"], writes=[pkey])
                if first:
                    P.add("dve", lambda e, cur=cur, pbank=pbank: e.tensor_copy(out=UB[cur][:], in_=pbank[:, :]), reads=[pkey], writes=[UK[cur]])
                else:
                    P.add("act", lambda e, prv=prv: e.activation(out=Sb[:], in_=UB[prv][:], func=AF.Copy, scale=g64),
                          reads=[UK[prv]], writes=["Sb"])
                    for vc in range(2):
                        for kc in range(2):
                            P.add("pe", lambda e, c=c, q2=q2, half=half, vc=vc, kc=kc: e.matmul(
                                M[1][:, vc * 256 + q2 * 128 + 64 * half:vc * 256 + q2 * 128 + 64 * half + 64],
                                lhsT=Sb[:, kc * 256 + vc * 128:kc * 256 + vc * 128 + 128], rhs=QR[kc][:, 64 * c:64 * c + 64],
                                start=False, stop=(kc == 1), skip_group_check=True),
                                reads=["Sb", ("B", 2), ("B", 3)], writes=[("m", 1)])
                    P.add("dve", lambda e, cur=cur, prv=prv, pbank=pbank: e.scalar_tensor_tensor(
                        out=UB[cur][:], in0=UB[prv][:], scalar=g64, in1=pbank[:, :], op0=ALU.mult, op1=ALU.add),
                        reads=[UK[prv], pkey], writes=[UK[cur]])
                if c == 15:
                    P.add("dve", lambda e, cur=cur: e.tensor_scalar(out=F0[:, 0:512], in0=UB[cur][:], scalar1=g64, scalar2=None, op0=ALU.mult),
                          reads=[UK[cur]], writes=[("F", 0)])
                if c == 16:
                    P.add("dve", lambda e: e.tensor_scalar(out=SLs[:], in0=Sf[:], scalar1=g64, scalar2=None, op0=ALU.mult),
                          reads=["Sf"], writes=["SLs"])
            if q2 == 1 or p == 8:
                g = p // 2
                nch = 4 if p < 8 else 1
                for vc in range(2):
                    src = M[1][:, vc * 256:vc * 256 + 256].rearrange("p (c s) -> p c s", s=64)[:, 0:nch, :]
                    P.add("dve", lambda e, vc=vc, g=g, nch=nch, src=src: e.tensor_tensor(
                        out=c3(OL[vc])[:, 4 * g:4 * g + nch, :], in0=src, in1=xi4[:, hb:hb + 1, :].to_broadcast([128, nch, 64]), op=ALU.mult),
                        reads=[("m", 1), "xi4"], writes=[("F", 2 + vc)])
        oc[0] += 0
        P.dma("sp", lambda e: e.dma_start(out=agB_in[l][hb].ap(), in_=F0[:, 0:512]), reads=[("F", 0)], writes=[("aginB", l, hb)], semkey="ag_st")
        P.dma("pool", lambda e: e.collective_compute("AllGather", ALU.bypass, replica_groups=[list(range(NCORE))],
                                                      ins=[agB_in[l][hb].ap()], outs=[agB_out[l][hb].ap()]),
              reads=[("aginB", l, hb)], writes=[("agoutB", l, hb)], semkey="cc", inc=1)
        Sin = F1[:, 0:512]
        for q in range(4):
            P.dma("sp", lambda e, q=q: e.dma_start(out=gat[:, :, 0:128],
                                                   in_=agB_out[l][hb].ap()[:, q * 128:(q + 1) * 128].rearrange("(r p) f -> p r f", p=128)),
                  reads=[("agoutB", l, hb)], writes=["gatA", "gatB"], semkey="ag_ld")
            P.add("dve", lambda e, q=q: e.tensor_scalar(out=Sin[:, q * 128:(q + 1) * 128], in0=gat[:, 0, 0:128], scalar1=coefB[:, hb, 0:1],
                                                        scalar2=None, op0=ALU.mult), reads=["gatA", "gatB", "coefB"], writes=[("F", 1)])
            for j in range(1, 7):
                P.add("dve", lambda e, q=q, j=j: e.scalar_tensor_tensor(
                    out=Sin[:, q * 128:(q + 1) * 128], in0=gat[:, j, 0:128], scalar=coefB[:, hb, j:j + 1], in1=Sin[:, q * 128:(q + 1) * 128],
                    op0=ALU.mult, op1=ALU.add), reads=["gatA", "gatB", "coefB", ("F", 1)], writes=[("F", 1)])
        P.add("act", lambda e: e.activation(out=Sinb[:], in_=Sin, func=AF.Copy), reads=[("F", 1)], writes=["Sinb"])
        P.dma("sp", lambda e: e.dma_start(out=stS[:].rearrange("p (kc v) -> p kc v", kc=2),
                                          in_=stb_d[l, hb].rearrange("(kc p) v -> p kc v", p=128)), writes=["gatB"], semkey="stS")
        P.add("act", lambda e: e.activation(out=Sinsb[:], in_=stS[:], func=AF.Copy), reads=["gatB"], writes=["Sinsb"])
        P.add("dve", lambda e: e.scalar_tensor_tensor(out=SO[:], in0=Sin, scalar=GAM[hb] ** 1024, in1=F0[:, 0:512], op0=ALU.mult, op1=ALU.add),
              reads=[("F", 1), ("F", 0)], writes=["gatA"])
        oc[0] += 1
        P.dma("sp", lambda e: e.dma_start(out=sob_d[l, 0, hb].rearrange("(kc p) v -> p kc v", p=128),
                                          in_=SO[:].rearrange("p (kc v) -> p kc v", kc=2)),
              reads=["gatA"], writes=[("out", oc[0]), "gatA"], semkey="so")
        P.add("dve", lambda e: e.scalar_tensor_tensor(out=SO[:], in0=stS[:], scalar=g64, in1=SLs[:], op0=ALU.mult, op1=ALU.add),
              reads=["gatB", "SLs"], writes=["gatA"])
        oc[0] += 1
        P.dma("sp", lambda e: e.dma_start(out=sob_d[l, 1, hb].rearrange("(kc p) v -> p kc v", p=128),
                                          in_=SO[:].rearrange("p (kc v) -> p kc v", kc=2)),
              reads=["gatA"], writes=[("out", oc[0]), "gatA"], semkey="so")
        GX = F1
        GX = F0
        P.add("dve", lambda e: e.tensor_tensor(out=c3(GX), in0=xi4[:, hb:hb + 1, :].to_broadcast([128, NCH, 64]),
                                               in1=gc4[:, hb, :].unsqueeze(2).to_broadcast([128, NCH, 64]), op=ALU.mult),
              reads=["xi4", "gc4"], writes=[("F", 0)])
        i = 0
        for vc in range(2):
            for tb, (t0, tn) in enumerate(TBS):
                m = i % 2
                i += 1
                lhs = Sinb if tb < 2 else Sinsb
                for kc in range(2):
                    P.add("pe", lambda e, t0=t0, tn=tn, m=m, lhs=lhs, kc=kc, vc=vc: e.matmul(
                        M[m][:, 0:tn], lhsT=lhs[:, kc * 256 + vc * 128:kc * 256 + vc * 128 + 128], rhs=QR[kc][:, t0:t0 + tn],
                        start=(kc == 0), stop=(kc == 1)), reads=["Sinb", "Sinsb", ("B", 2), ("B", 3)], writes=[("m", m)])
                tmp = Bv(0)[:, 0:1024].bitcast(F32)
                P.add("dve", lambda e, t0=t0, tn=tn, m=m, tmp=tmp: e.tensor_tensor(out=tmp[:, 0:tn], in0=M[m][:, 0:tn], in1=GX[:, t0:t0 + tn], op=ALU.mult),
                      reads=[("m", m), ("F", 0)], writes=[("B", 0)])
                P.add("dve", lambda e, t0=t0, tn=tn, vc=vc, tmp=tmp: e.tensor_tensor(out=OL[vc][:, t0:t0 + tn], in0=OL[vc][:, t0:t0 + tn], in1=tmp[:, 0:tn], op=ALU.add),
                      reads=[("B", 0), ("F", 2 + vc)], writes=[("F", 2 + vc)])
        sumsq_norm(OL, [("F", 2), ("F", 3)], [grnT[:, l, 2 * hb:2 * hb + 1], grnT[:, l, 2 * hb + 1:2 * hb + 2]],
                   [B[6], B[7]], [("B", 6), ("B", 7)], [2 * hb, 2 * hb + 1], 256)

    def wout_group(l, gi):
        for cb in range(8):
            s = wload(wsrc(w_out, l, 512 * gi, 4, 256 * cb), 4)
            for cc in range(2):
                fc = 2 * cb + cc
                ps = proj(lambda kc, cc=cc, s=s: wsl[:, s, kc, cc * 128:(cc + 1) * 128], 4,
                          lambda kc, t0, tn: oTg[:, kc, t0:t0 + tn], [("w", s)] + [("og", k) for k in range(4)])
                for tb, (t0, tn) in enumerate(TBS):
                    sg = 0 if tb < 2 else 1
                    P.add("dve", lambda e, fc=fc, tb=tb, t0=t0, tn=tn, sg=sg, ps=ps: e.scalar_tensor_tensor(
                        out=xT[:, fc, t0:t0 + tn], in0=PJ[ps][tb][:, 0:tn], scalar=MOD[sg][:, l, 32 + fc:33 + fc], in1=xT[:, fc, t0:t0 + tn],
                        op0=ALU.mult, op1=ALU.add), reads=[("pj", ps, tb), ("x", fc)] + MODK, writes=[("x", fc)])

    def ffn(l):
        for j in range(4):
            for blk in range(8):
                s = wload(wsrc(w_up, l, 0, 16, 2048 * j + 256 * blk), 16)
                for cc in range(2):
                    ufc = 2 * blk + cc
                    ps = proj(lambda kc, cc=cc, s=s: wsl[:, s, kc, cc * 128:(cc + 1) * 128], 16, hrhs, [("w", s), "h"])
                    tmpk, tkey = (K0f, "K0") if ufc % 2 == 0 else (K1f, "K1")
                    evac_act(ps, lambda t0, tn, tmpk=tmpk: tmpk[:, t0:t0 + tn], AF.Relu, [tkey])
                    P.add("dve", lambda e, ufc=ufc, tmpk=tmpk: e.tensor_tensor(out=uT[:, ufc, :], in0=tmpk, in1=tmpk, op=ALU.mult),
                          reads=[tkey], writes=[ukey(ufc)])
            ukeys = sorted(set(ukey(f) for f in range(16)))
            for cb in range(8):
                s = wload(wsrc(w_down, l, 2048 * j, 16, 256 * cb), 16)
                for cc in range(2):
                    fc = 2 * cb + cc
                    ps = proj(lambda kc, cc=cc, s=s: wsl[:, s, kc, cc * 128:(cc + 1) * 128], 16,
                              lambda kc, t0, tn: uT[:, kc, t0:t0 + tn], [("w", s)] + ukeys)
                    for tb, (t0, tn) in enumerate(TBS):
                        sg = 0 if tb < 2 else 1
                        P.add("dve", lambda e, fc=fc, tb=tb, t0=t0, tn=tn, sg=sg, ps=ps: e.scalar_tensor_tensor(
                            out=xT[:, fc, t0:t0 + tn], in0=PJ[ps][tb][:, 0:tn], scalar=MOD[sg][:, l, 80 + fc:81 + fc], in1=xT[:, fc, t0:t0 + tn],
                            op0=ALU.mult, op1=ALU.add), reads=[("pj", ps, tb), ("x", fc)] + MODK, writes=[("x", fc)])

    def finish():
        P.add("sp", None, reads=[("out", k) for k in range(1, oc[0] + 1)])
        for eng_ in ("pe", "act", "dve", "pool"):
            pass
        P.emit(nc, st)
        st.close()
        return nc

    for l in range(L if DBG_STOP != 9 else 0):
        if DBG_STOP == 2:
            return finish()
        norm(l, A1, 0)
        if DBG_STOP == 3:
            return finish()
        slots = hgrn_load(l, 0)
        for hd in range(8):
            hgrn_head(l, hd, "front", slots)
            if hd >= 1:
                hgrn_head(l, hd - 1, "tail")
                if (hd - 1) % 4 == 3:
                    wout_group(l, (hd - 1) // 4)
            if hd < 7:
                slots = hgrn_load(l, hd + 1)
            hgrn_head(l, hd, "chain")
        hgrn_head(l, 7, "tail")
        wout_group(l, 1)
        if DBG_STOP == 5:
            return finish()
        for hb in range(4):
            ret_head(l, hb)
            if DBG_STOP == 6:
                return finish()
            if hb % 2 == 1:
                wout_group(l, 2 + hb // 2)
        if DBG_STOP == 7:
            return finish()
        norm(l, A2, 48)
        ffn(l)
        if DBG_STOP == 8:
            return finish()

    import os
    SUB = int(os.environ.get('DBG_SUB', '0'))
    if SUB != 3:
        norm(0, None, 0, final=True)
    for ti in range(NPAIR if SUB in (0, 3) else (0 if SUB == 1 else 8)):
        tok0 = 128 * ti if ti < 8 else T - 128
        r0 = 0 if ti < 8 else 64
        for g in range(4):
            s = next_set()
            for q in range(4):
                fc = 4 * g + q
                P.add("pe", lambda e, s=s, q=q, fc=fc, tok0=tok0: e.transpose(
                    out=PJ[s][0][:, q * 128:(q + 1) * 128], in_=xT[:, fc, tok0:tok0 + 128], identity=identf[:]),
                    reads=[("x", fc), "identf"], writes=[("pj", s, 0)])
            P.add("act", lambda e, s=s, g=g, r0=r0: e.activation(out=stage[r0:128, 512 * g:512 * g + 512], in_=PJ[s][0][r0:128, :], func=AF.Copy),
                  reads=[("pj", s, 0)], writes=[("F", 0), ("F", 1)])
        oc[0] += 1
        P.dma("sp", lambda e, tok0=tok0, r0=r0: e.dma_start(out=y_d[tok0 + r0:tok0 + 128, :], in_=stage[r0:128, :]),
              reads=[("F", 0), ("F", 1)], writes=[("out", oc[0]), ("F", 0), ("F", 1)], semkey="yout")
    P.add("sp", None, reads=[("out", k) for k in range(1, oc[0] + 1)])
    P.emit(nc, st)
    st.close()
    return nc


_NC = None


def _host_tables(i):
    half = 128
    inv_freq = (1.0 / (10000.0 ** np.linspace(0.0, 1.0, half, dtype=np.float32))).astype(np.float32)
    pos = np.concatenate([1024 * i + np.arange(TP), 2048 + np.arange(64)]).astype(np.float32)
    ang = (pos[:, None] * inv_freq[None, :]).astype(np.float32)
    cosT = np.ascontiguousarray(np.cos(ang).astype(np.float32).T).astype(ml_dtypes.bfloat16)
    sinT = np.ascontiguousarray(np.sin(ang).astype(np.float32).T).astype(ml_dtypes.bfloat16)
    m = np.zeros((128, 128), np.uint8)
    for b in range(2):
        m[64 * b:64 * b + 64, 64 * b:64 * b + 64] = np.triu(np.ones((64, 64), np.uint8))
    maskrep = np.ascontiguousarray(np.broadcast_to(m[:, None, :], (128, 4, 128)))
    gam = np.array(GAM, np.float64)
    s = np.arange(64)
    xi4 = np.broadcast_to((gam[:, None] ** (s[None, :] + 1)).astype(np.float32)[None], (128, 4, 64)).copy()
    gc = np.ones((4, NCH), np.float64)
    gc[:, :16] = gam[:, None] ** (64.0 * np.arange(16)[None, :])
    gc4 = np.broadcast_to(gc.astype(np.float32)[None], (128, 4, NCH)).copy()
    pp = np.arange(128) % 64
    zinv = ((gam[None, :] ** (-(pp[:, None] + 1.0))) / 16.0).astype(np.float32)
    maskA = np.broadcast_to((np.arange(8) < i).astype(np.float32)[None], (128, 8)).copy()
    cb = np.zeros((4, 8), np.float64)
    for j in range(8):
        if j < i:
            cb[:, j] = gam ** (1024.0 * (i - 1 - j))
    coefB = np.broadcast_to(cb.astype(np.float32)[None], (128, 4, 8)).copy()
    oh = np.zeros((128, 9), np.float32)
    oh[:, 1 + i] = 1.0
    return dict(cosT=cosT, sinT=sinT, maskrep=maskrep, xi4=xi4, gc4=gc4, zinv=zinv, maskA=maskA, coefB=coefB, onehot=oh)


def kernel(x_prompt, x_sample, state_hgrn, state_ret, c_prompt, c_sample, lb_logits, w_ada, b_ada,
           norm1_g, norm2_g, w_in, hgrn_norm_g, ret_norm_g, w_out, w_up, w_down, final_g):
    global _NC
    f = lambda a: np.ascontiguousarray(np.asarray(a, dtype=np.float32))
    x_prompt, x_sample, state_hgrn, state_ret = f(x_prompt), f(x_sample), f(state_hgrn), f(state_ret)
    w_ada, w_in, w_out, w_up, w_down = f(w_ada), f(w_in), f(w_out), f(w_up), f(w_down)
    idx = []
    for hd in range(8):
        for part in range(4):
            idx.append(np.arange(1024 * part + 128 * hd, 1024 * part + 128 * hd + 128))
    for hb in range(4):
        for part in range(4):
            idx.append(np.arange(4096 + 1024 * part + 256 * hb, 4096 + 1024 * part + 256 * hb + 256))
    idx = np.concatenate(idx)
    w_in_p = np.ascontiguousarray(w_in[:, :, idx])
    c_all = np.concatenate([f(c_prompt), f(c_sample)], 0)
    cT = np.ascontiguousarray(c_all.reshape(9, 16, 128).transpose(2, 1, 0))
    vT = lambda v, n: np.ascontiguousarray(f(v).reshape(L, n, 128).transpose(2, 0, 1))
    g1T, g2T = vT(norm1_g, 16), vT(norm2_g, 16)
    gfT = np.ascontiguousarray(f(final_g).reshape(16, 128).T)
    ghnT, grnT = vT(hgrn_norm_g, 8), vT(ret_norm_g, 8)
    lblT = np.ascontiguousarray(f(lb_logits).reshape(L, 8, 128).transpose(2, 1, 0))
    identb = np.eye(128).astype(ml_dtypes.bfloat16)
    identf = np.eye(128).astype(np.float32)
    b_ada = f(b_ada)
    if _NC is None or _NC[0] != L:
        _NC = (L, build_program())
    in_maps = []
    tiny = np.zeros((1, 128, 256), np.float32)
    for i in range(NCORE):
        d = dict(
            xin=np.ascontiguousarray(np.concatenate([x_prompt[0, 1024 * i:1024 * (i + 1)], x_sample[i]], 0)),
            w_in=w_in_p if _need("w_in") else tiny, w_out=w_out if _need("w_out") else tiny,
            w_up=w_up if _need("w_up") else tiny, w_down=w_down if _need("w_down") else tiny,
            w_ada=np.ascontiguousarray(w_ada[:, :, 1536 * i:1536 * (i + 1)]),
            cT=cT, bT=np.ascontiguousarray(b_ada[:, 1536 * i:1536 * (i + 1)].reshape(L, 12, 128).transpose(2, 0, 1)),
            g1T=g1T, g2T=g2T, gfT=gfT, ghnT=ghnT, grnT=grnT, lblT=lblT, identb=identb, identf=identf,
            st_a=np.ascontiguousarray(state_hgrn[:, i]), st_b=np.ascontiguousarray(state_ret[:, i]),
        )
        d.update(_host_tables(i))
        in_maps.append(d)
    res = run_bass_kernel_spmd(_NC[1], in_maps, core_ids=list(range(NCORE)))
    r = res.results
    y_prompt = np.concatenate([r[i]["y"][:TP] for i in range(NCORE)], 0)[None]
    y_sample = np.stack([r[i]["y"][TP:] for i in range(NCORE)], 0)
    sa_p = r[NCORE - 1]["so_a"][:, 0][:, None]
    sb_p = r[NCORE - 1]["so_b"][:, 0][:, None]
    sa_s = np.stack([r[i]["so_a"][:, 1] for i in range(NCORE)], 1)
    sb_s = np.stack([r[i]["so_b"][:, 1] for i in range(NCORE)], 1)
    return (y_prompt.astype(np.float32), y_sample.astype(np.float32), sa_p.astype(np.float32), sb_p.astype(np.float32),
            sa_s.astype(np.float32), sb_s.astype(np.float32))
```

```python
import numpy as np
import ml_dtypes
from contextlib import ExitStack
import concourse.bass as bass
import concourse.mybir as mybir
from concourse.bass_utils import run_bass_kernel_spmd

F32 = mybir.dt.float32
BF16 = mybir.dt.bfloat16
U8 = mybir.dt.uint8
AF = mybir.ActivationFunctionType
ALU = mybir.AluOpType
AX = mybir.AxisListType

NCORE = 8
T = 1088
TP = 1024
NCH = 17
NPAIR = 9
L = 4
TBS = [(0, 512), (512, 512), (1024, 64)]
EPS = 1e-6
DBG_STOP = None
GAM = [1.0 - 2.0 ** (-5 - h) for h in range(4)]


class _Op:
    __slots__ = ("eng", "fn", "deps", "kind", "semkey", "idx", "sig", "cnt", "eidx", "inc")


class Prog:
    ENGS = ("pe", "act", "dve", "pool", "sp")

    def __init__(self):
        self.ops = []
        self.lastw = {}
        self.readers = {}

    def add(self, eng, fn, reads=(), writes=(), kind="c", semkey=None, inc=16):
        i = len(self.ops)
        deps = set()
        for r in reads:
            w = self.lastw.get(r)
            if w is not None:
                deps.add(w)
        for w_ in writes:
            w = self.lastw.get(w_)
            if w is not None:
                deps.add(w)
            for rd in self.readers.get(w_, {}).values():
                deps.add(rd)
        rk = ("dma", i) if kind == "dma" else eng
        for r in reads:
            self.readers.setdefault(r, {})[rk] = i
        for w_ in writes:
            self.lastw[w_] = i
            self.readers[w_] = {}
        op = _Op()
        op.eng, op.fn, op.deps, op.kind, op.semkey, op.idx = eng, fn, deps, kind, semkey, i
        op.sig, op.cnt, op.eidx, op.inc = False, 0, 0, inc
        self.ops.append(op)
        return i

    def dma(self, eng, fn, reads=(), writes=(), semkey=None, inc=16):
        return self.add(eng, fn, reads, writes, kind="dma", semkey=semkey, inc=inc)

    def emit(self, nc, stack):
        ops = self.ops
        nper = {e: 0 for e in self.ENGS}
        for op in ops:
            op.eidx = nper[op.eng]
            nper[op.eng] += 1
        waits = []
        for op in ops:
            wl = []
            for d in op.deps:
                dop = ops[d]
                if dop.kind == "dma":
                    wl.append(d)
                    continue
                if dop.eng == op.eng:
                    if op.eng == "pe":
                        continue
                    if op.kind != "dma" and (op.eidx - dop.eidx) > 2:
                        continue
                dop.sig = True
                wl.append(d)
            waits.append(wl)
        ecount = {e: 0 for e in self.ENGS}
        dcount = {}
        for op in ops:
            if op.kind == "dma":
                dcount[op.semkey] = dcount.get(op.semkey, 0) + op.inc
                op.cnt = dcount[op.semkey]
            elif op.sig:
                ecount[op.eng] += 1
                op.cnt = ecount[op.eng]
        print("signal counts", ecount, "dma", {k: v for k, v in dcount.items() if v > 256}, "nops", nper)
        esem = {e: stack.enter_context(nc.semaphore("s_" + e)) for e in self.ENGS}
        dsem = {k: stack.enter_context(nc.semaphore("d_%s" % (k,))) for k in dcount}
        per_eng = {e: [] for e in self.ENGS}
        for op in ops:
            per_eng[op.eng].append(op)

        def run(engname, eng):
            waited = {}
            for op in per_eng[engname]:
                need = {}
                for d in waits[op.idx]:
                    dop = ops[d]
                    if dop.kind == "dma":
                        key = ("d", dop.semkey)
                        sem = dsem[dop.semkey]
                    else:
                        key = ("e", dop.eng)
                        sem = esem[dop.eng]
                    if dop.cnt > need.get(key, (None, 0))[1]:
                        need[key] = (sem, dop.cnt)
                for key, (sem, cnt) in need.items():
                    if waited.get(key, 0) >= cnt:
                        continue
                    eng.wait_ge(sem, cnt)
                    waited[key] = cnt
                if op.fn is None:
                    continue
                ins = op.fn(eng)
                if op.kind == "dma":
                    ins.then_inc(dsem[op.semkey], op.inc)
                elif op.sig:
                    ins.then_inc(esem[op.eng], 1)

        block = stack.enter_context(nc.Block())

        @block.sync
        def _(e):
            run("sp", e)

        @block.tensor
        def _(e):
            run("pe", e)

        @block.scalar
        def _(e):
            run("act", e)

        @block.vector
        def _(e):
            run("dve", e)

        @block.gpsimd
        def _(e):
            run("pool", e)


def _need(name):
    if DBG_STOP is None:
        return True
    if DBG_STOP == 9:
        return False
    if name in ("w_in", "w_out"):
        return DBG_STOP >= 4
    return DBG_STOP >= 8


def build_program():
    nc = bass.Bass("TRN2", target_bir_lowering=False)
    P = Prog()

    def din(name, shape, dt=F32):
        return nc.dram_tensor(name, list(shape), dt, kind="ExternalInput").ap()

    def dout(name, shape, dt=F32):
        return nc.dram_tensor(name, list(shape), dt, kind="ExternalOutput").ap()

    xin = din("xin", [T, 2048])
    w_in = din("w_in", [L, 2048, 8192] if _need("w_in") else [1, 128, 256])
    w_out = din("w_out", [L, 2048, 2048] if _need("w_out") else [1, 128, 256])
    w_up = din("w_up", [L, 2048, 8192] if _need("w_up") else [1, 128, 256])
    w_down = din("w_down", [L, 8192, 2048] if _need("w_down") else [1, 128, 256])
    w_ada = din("w_ada", [L, 2048, 1536])
    cT_d = din("cT", [128, 16, 9])
    bT_d = din("bT", [128, L, 12])
    g1_d = din("g1T", [128, L, 16])
    g2_d = din("g2T", [128, L, 16])
    gf_d = din("gfT", [128, 16])
    ghn_d = din("ghnT", [128, L, 8])
    grn_d = din("grnT", [128, L, 8])
    lbl_d = din("lblT", [128, 8, L])
    cos_d = din("cosT", [128, T], BF16)
    sin_d = din("sinT", [128, T], BF16)
    mask_d = din("maskrep", [128, 4, 128], U8)
    xi_d = din("xi4", [128, 4, 64])
    gc_d = din("gc4", [128, 4, NCH])
    zinv_d = din("zinv", [128, 4])
    mA_d = din("maskA", [128, 8])
    cB_d = din("coefB", [128, 4, 8])
    oh_d = din("onehot", [128, 9])
    idb_d = din("identb", [128, 128], BF16)
    idf_d = din("identf", [128, 128])
    sta_d = din("st_a", [L, 8, 128, 128])
    stb_d = din("st_b", [L, 4, 256, 256])
    y_d = dout("y", [T, 2048])
    soa_d = dout("so_a", [L, 2, 8, 128, 128])
    sob_d = dout("so_b", [L, 2, 4, 256, 256])

    ag_mod_in = nc.dram_tensor("ag_mod_in", [128, 108 * L], F32)
    ag_mod_out = nc.dram_tensor("ag_mod_out", [1024, 108 * L], F32)
    agA_in = [[nc.dram_tensor("agA_in_%d_%d" % (l, h), [128, 129], F32) for h in range(8)] for l in range(L)]
    agA_out = [[nc.dram_tensor("agA_out_%d_%d" % (l, h), [1024, 129], F32) for h in range(8)] for l in range(L)]
    agB_in = [[nc.dram_tensor("agB_in_%d_%d" % (l, h), [128, 512], F32) for h in range(4)] for l in range(L)]
    agB_out = [[nc.dram_tensor("agB_out_%d_%d" % (l, h), [1024, 512], F32) for h in range(4)] for l in range(L)]

    st = ExitStack()

    def sb(name, shape, dt=F32):
        return st.enter_context(nc.sbuf_tensor("sb_" + name, list(shape), dt))

    xT = sb("xT", [128, 16, T])
    hT = sb("hT", [128, 16, T], BF16)
    wsl = sb("wsl", [128, 3, 16, 256], BF16)
    arena = sb("arena", [128, 18496], BF16)
    oTg = sb("oTg", [128, 4, T], BF16)
    K0 = sb("K0", [128, NPAIR, 256], BF16)
    K1 = sb("K1", [128, NPAIR, 256], BF16)
    ATb = sb("ATb", [128, NPAIR, 128], BF16)
    gat = sb("gat", [128, 8, 129])
    Sf = sb("Sf", [128, 512])
    SLs = sb("SLs", [128, 512])
    Sb = sb("Sb", [128, 512], BF16)
    Sinb = sb("Sinb", [128, 512], BF16)
    Sinsb = sb("Sinsb", [128, 512], BF16)
    GI = sb("GI", [128, 129])
    cosT = sb("cosT", [128, T], BF16)
    sinT = sb("sinT", [128, T], BF16)
    maskrep = sb("maskrep", [128, 4, 128], U8)
    xi4 = sb("xi4", [128, 4, 64])
    gc4 = sb("gc4", [128, 4, NCH])
    zinv = sb("zinv", [128, 4])
    maskA = sb("maskA", [128, 8])
    coefB = sb("coefB", [128, 4, 8])
    onehot = sb("onehot", [128, 9])
    identb = sb("identb", [128, 128], BF16)
    identf = sb("identf", [128, 128])
    onesb = sb("onesb", [128, 128], BF16)
    zcol = sb("zcol", [128, 1])
    cact = sb("cact", [128, 16, 9], BF16)
    bTs = sb("bTs", [128, L, 12])
    g1T = sb("g1T", [128, L, 16])
    g2T = sb("g2T", [128, L, 16])
    gfT = sb("gfT", [128, 16])
    ghnT = sb("ghnT", [128, L, 8])
    grnT = sb("grnT", [128, L, 8])
    lbl = sb("lbl", [128, 8, L])
    lbv = sb("lbv", [128, L, 8])
    omlv = sb("omlv", [128, L, 8])
    lbtmp = sb("lbtmp", [128, 8, 2])
    MOD = [sb("MOD%d" % s, [128, L, 96]) for s in range(2)]
    A1 = [sb("A1_%d" % s, [128, L, 16]) for s in range(2)]
    A2 = [sb("A2_%d" % s, [128, L, 16]) for s in range(2)]
    sm = sb("sm", [128, 4, NCH])
    dexp = sb("dexp", [128, 4, NCH])
    acoef = sb("acoef", [128, 8])
    epst = sb("epst", [128, 4])
    dts = sb("dts", [128, 2, 2])
    gatf = gat[:].rearrange("p a b -> p (a b)")
    SO = gatf[:, 0:512]
    stS = gatf[:, 512:1024]
    modp = gatf[:, 0:108 * L].rearrange("p (a b) -> p a b", b=9)
    cTs = gatf[:, 432:576].rearrange("p (a b) -> p a b", b=9)
    print("sbuf remaining", nc.sbuf_bytes_remaining)

    PS = [st.enter_context(nc.psum_tensor("ps%d" % b, [128, 512], F32)) for b in range(8)]
    PJ = [[PS[0], PS[1], PS[2]], [PS[3], PS[4], PS[5]]]
    M = [PS[6], PS[7]]
    Mb = [PS[6][:, :].bitcast(BF16), PS[7][:, :].bitcast(BF16)]

    def Fv(k):
        return arena[:, 2176 * k:2176 * (k + 1)].bitcast(F32)

    def Bv(k):
        return arena[:, 8704 + 1088 * k:8704 + 1088 * (k + 1)]

    uT = arena[:, 0:17408].rearrange("p (c t) -> p c t", t=T)

    def ukey(fc):
        return ("F", fc // 2) if fc < 8 else ("B", fc - 8)

    K0f = K0[:].rearrange("p a b -> p (a b)")[:, 0:2176].bitcast(F32)
    K1f = K1[:].rearrange("p a b -> p (a b)")[:, 0:2176].bitcast(F32)
    stage = arena[:, 0:4096].bitcast(F32)
    modall = arena[:, 0:1728 * L].bitcast(F32)

    def c3(ap):
        return ap.rearrange("p (c s) -> p c s", s=64)

    state = {"pjset": 0, "wslot": 0}

    def next_set():
        s = state["pjset"]
        state["pjset"] ^= 1
        return s

    def wload(src_ap, nkc):
        s = state["wslot"]
        state["wslot"] = (s + 1) % 3
        P.dma("pool", lambda e, s=s, src_ap=src_ap, nkc=nkc: e.dma_start(out=wsl[:, s, 0:nkc, :], in_=src_ap),
              writes=[("w", s)], semkey="w%d" % s)
        return s

    def wsrc(w, l, r0, nkc, c0):
        return w[l, r0:r0 + 128 * nkc, c0:c0 + 256].rearrange("(kc p) n -> p kc n", p=128)

    def proj(lhs_fn, nk, rhs_fn, reads, tbs=TBS):
        s = next_set()
        hk = "h" in reads
        base = [r for r in reads if r != "h"]
        for kc in range(nk):
            rd = base + ([("h", kc)] if hk else [])
            for tb, (t0, tn) in enumerate(tbs):
                P.add("pe", lambda e, s=s, kc=kc, tb=tb, t0=t0, tn=tn: e.matmul(
                    PJ[s][tb][:, 0:tn], lhsT=lhs_fn(kc), rhs=rhs_fn(kc, t0, tn), start=(kc == 0), stop=(kc == nk - 1)),
                    reads=rd, writes=[("pj", s, tb)])
        return s

    def evac_act(s, dst_fn, func, writes, scale=1.0, extra_reads=()):
        for tb, (t0, tn) in enumerate(TBS):
            P.add("act", lambda e, s=s, tb=tb, t0=t0, tn=tn: e.activation(
                out=dst_fn(t0, tn), in_=PJ[s][tb][:, 0:tn], func=func, scale=scale),
                reads=[("pj", s, tb)] + list(extra_reads), writes=writes)

    def hrhs(kc, t0, tn):
        return hT[:, kc, t0:t0 + tn]

    cnt = [0]

    def ld(dst, src, key):
        cnt[0] += 1
        P.dma("sp", lambda e: e.dma_start(out=dst, in_=src), writes=[key], semkey="c%d" % cnt[0])

    for dst, src, key in [
        (cTs, cT_d, "gatA"), (bTs[:], bT_d, "bTs"), (g1T[:], g1_d, "g1T"), (g2T[:], g2_d, "g2T"), (gfT[:], gf_d, "gfT"),
        (ghnT[:], ghn_d, "ghnT"), (grnT[:], grn_d, "grnT"), (lbl[:], lbl_d, "lbl"), (cosT[:], cos_d, "cos"),
        (sinT[:], sin_d, "sin"), (maskrep[:], mask_d, "maskrep"), (xi4[:], xi_d, "xi4"), (gc4[:], gc_d, "gc4"),
        (zinv[:], zinv_d, "zinv"), (maskA[:], mA_d, "maskA"), (coefB[:], cB_d, "coefB"), (onehot[:], oh_d, "onehot"),
        (identb[:], idb_d, "identb"), (identf[:], idf_d, "identf"),
    ]:
        ld(dst, src, key)
    P.add("pool", lambda e: e.memset(onesb[:], 1.0), writes=["onesb"])
    P.add("pool", lambda e: e.memset(zcol[:], 0.0), writes=["zcol"])
    P.add("pool", lambda e: e.memset(epst[:, 0:1], 2048.0 * EPS), writes=["epst"])
    P.add("pool", lambda e: e.memset(epst[:, 1:2], 128.0 * EPS), writes=["epst"])
    P.add("pool", lambda e: e.memset(epst[:, 2:3], 256.0 * EPS), writes=["epst"])
    P.add("pool", lambda e: e.memset(ATb[:], 0.0), writes=["ATb"])

    P.add("dve", lambda e: e.tensor_reduce(out=lbtmp[:, :, 0], in_=lbl[:], axis=AX.X, op=ALU.max), reads=["lbl"], writes=["lbt0"])
    P.add("dve", lambda e: e.tensor_tensor(out=lbl[:], in0=lbl[:], in1=lbtmp[:, :, 0:1].to_broadcast([128, 8, L]), op=ALU.subtract),
          reads=["lbl", "lbt0"], writes=["lbl"])
    P.add("act", lambda e: e.activation(out=lbl[:], in_=lbl[:], func=AF.Exp), reads=["lbl"], writes=["lbl"])
    P.add("dve", lambda e: e.tensor_reduce(out=lbtmp[:, :, 1], in_=lbl[:], axis=AX.X, op=ALU.add), reads=["lbl"], writes=["lbt1"])
    P.add("dve", lambda e: e.reciprocal(out=lbtmp[:, :, 1], in_=lbtmp[:, :, 1]), reads=["lbt1"], writes=["lbt1"])
    P.add("dve", lambda e: e.tensor_tensor(out=lbl[:], in0=lbl[:], in1=lbtmp[:, :, 1:2].to_broadcast([128, 8, L]), op=ALU.mult),
          reads=["lbl", "lbt1"], writes=["lbl"])
    P.add("dve", lambda e: e.memset(lbv[:, 0, :], 0.0), reads=[], writes=["lbv"])
    for l in range(1, L):
        P.add("dve", lambda e, l=l: e.tensor_tensor(out=lbv[:, l, :], in0=lbv[:, l - 1, :], in1=lbl[:, :, l], op=ALU.add),
              reads=["lbl", "lbv"], writes=["lbv"])
    P.add("dve", lambda e: e.tensor_scalar(out=omlv[:], in0=lbv[:], scalar1=-1.0, scalar2=1.0, op0=ALU.mult, op1=ALU.add),
          reads=["lbv"], writes=["omlv"])

    P.add("act", lambda e: e.activation(out=cact[:], in_=cTs, func=AF.Silu), reads=["gatA", "gatB"], writes=["cact"])
    for l in range(L):
        for jb in range(6):
            s = wload(wsrc(w_ada, l, 0, 16, 256 * jb), 16)
            for cc in range(2):
                j = 2 * jb + cc
                ps = next_set()
                for kc in range(16):
                    P.add("pe", lambda e, s=s, kc=kc, cc=cc, ps=ps: e.matmul(
                        PJ[ps][0][:, 0:9], lhsT=wsl[:, s, kc, cc * 128:(cc + 1) * 128], rhs=cact[:, kc, :],
                        start=(kc == 0), stop=(kc == 15)), reads=[("w", s), "cact"], writes=[("pj", ps, 0)])
                P.add("act", lambda e, l=l, j=j, ps=ps: e.activation(
                    out=modp[:, l * 12 + j, :], in_=PJ[ps][0][:, 0:9], func=AF.Identity, bias=bTs[:, l, j:j + 1]),
                    reads=[("pj", ps, 0), "bTs"], writes=["gatA", "gatB"])
    P.dma("sp", lambda e: e.dma_start(out=ag_mod_in.ap(), in_=gatf[:, 0:108 * L]),
          reads=["gatA", "gatB"], writes=["ag_mod_in"], semkey="agm_st")
    P.dma("pool", lambda e: e.collective_compute("AllGather", ALU.bypass, replica_groups=[list(range(NCORE))],
                                                  ins=[ag_mod_in.ap()], outs=[ag_mod_out.ap()]),
          reads=["ag_mod_in"], writes=["ag_mod_out"], semkey="cc", inc=1)
    FKEYS = [("F", 0), ("F", 1), ("F", 2), ("F", 3)]
    P.dma("sp", lambda e: e.dma_start(out=modall.rearrange("p (r f) -> p r f", r=8),
                                      in_=ag_mod_out.ap().rearrange("(r p) f -> p r f", p=128)),
          reads=["ag_mod_out"], writes=FKEYS, semkey="agm_ld")
    ma5 = modall.rearrange("p (r l j w) -> p r l j w", r=8, l=L, j=12)
    for r in range(8):
        P.add("dve", lambda e, r=r: e.tensor_copy(out=MOD[0][:, :, 12 * r:12 * r + 12], in_=ma5[:, r, :, :, 0]),
              reads=FKEYS, writes=["MOD0"])
    ma3 = modall.rearrange("p (a w) -> p a w", w=9)
    P.add("dve", lambda e: e.tensor_tensor(out=ma3, in0=ma3, in1=onehot[:].unsqueeze(1).to_broadcast([128, 96 * L, 9]), op=ALU.mult),
          reads=FKEYS + ["onehot", "MOD0"], writes=FKEYS)
    msel = Bv(0)[:, 0:192 * L].bitcast(F32)
    P.add("dve", lambda e: e.tensor_reduce(out=msel, in_=ma3, axis=AX.X, op=ALU.add), reads=FKEYS, writes=[("B", 0)])
    ms4 = msel.rearrange("p (r l j) -> p r l j", r=8, l=L)
    for r in range(8):
        P.add("dve", lambda e, r=r: e.tensor_copy(out=MOD[1][:, :, 12 * r:12 * r + 12], in_=ms4[:, r, :, :]),
              reads=[("B", 0)], writes=["MOD1"])
    for sg in range(2):
        P.add("dve", lambda e, sg=sg: e.scalar_tensor_tensor(out=A1[sg][:], in0=MOD[sg][:, :, 16:32], scalar=1.0, in1=g1T[:],
                                                               op0=ALU.add, op1=ALU.mult),
              reads=["MOD%d" % sg, "g1T"], writes=["A1_%d" % sg])
        P.add("dve", lambda e, sg=sg: e.scalar_tensor_tensor(out=A2[sg][:], in0=MOD[sg][:, :, 64:80], scalar=1.0, in1=g2T[:],
                                                               op0=ALU.add, op1=ALU.mult),
              reads=["MOD%d" % sg, "g2T"], writes=["A2_%d" % sg])
    MODK = ["MOD0", "MOD1"]

    XK = [("x", fc) for fc in range(16)]
    for ti in range(NPAIR):
        n = 128 if ti < 8 else 64
        P.dma("sp", lambda e, ti=ti, n=n: e.dma_start(out=stage[0:n, :], in_=xin[128 * ti:128 * ti + n, :]),
              writes=[("F", 0), ("F", 1)], semkey="xld")
        for g in range(4):
            s = next_set()
            for q in range(4):
                fc = 4 * g + q
                P.add("pe", lambda e, s=s, q=q, fc=fc, n=n: e.transpose(
                    out=PJ[s][0][:, q * 128:q * 128 + n], in_=stage[0:n, fc * 128:(fc + 1) * 128], identity=identf[0:n, 0:n]),
                    reads=[("F", 0), ("F", 1), "identf"], writes=[("pj", s, 0)])
            P.add("act", lambda e, s=s, g=g, ti=ti, n=n: e.activation(
                out=xT[:, 4 * g:4 * g + 4, 128 * ti:128 * ti + n],
                in_=PJ[s][0][:, :].rearrange("p (q t) -> p q t", q=4)[:, :, 0:n], func=AF.Copy),
                reads=[("pj", s, 0)], writes=[("x", 4 * g + q) for q in range(4)])

    def norm(l, Acoef, Boff, final=False):
        s = next_set()
        for fc in range(16):
            sq = Bv(fc % 2)
            P.add("act", lambda e, fc=fc, sq=sq: e.activation(out=sq, in_=xT[:, fc, :], func=AF.Square),
                  reads=[("x", fc)], writes=[("B", fc % 2)])
            for tb, (t0, tn) in enumerate(TBS):
                P.add("pe", lambda e, fc=fc, sq=sq, tb=tb, t0=t0, tn=tn, s=s: e.matmul(
                    PJ[s][tb][:, 0:tn], lhsT=onesb[:], rhs=sq[:, t0:t0 + tn], start=(fc == 0), stop=(fc == 15)),
                    reads=[("B", fc % 2), "onesb"], writes=[("pj", s, tb)])
        rstd = Fv(3)
        for tb, (t0, tn) in enumerate(TBS):
            P.add("act", lambda e, tb=tb, t0=t0, tn=tn, s=s: e.activation(
                out=rstd[:, t0:t0 + tn], in_=PJ[s][tb][:, 0:tn], func=AF.Sqrt, bias=epst[:, 0:1]),
                reads=[("pj", s, tb), "epst"], writes=[("F", 3)])
        P.add("dve", lambda e: e.reciprocal(out=rstd, in_=rstd), reads=[("F", 3)], writes=[("F", 3)])
        rt = 2048.0 ** 0.5
        for fc in range(16):
            tmp = Fv(fc % 2 + 1)
            if final:
                P.add("dve", lambda e, fc=fc, tmp=tmp: e.scalar_tensor_tensor(
                    out=tmp, in0=xT[:, fc, :], scalar=gfT[:, fc:fc + 1], in1=rstd, op0=ALU.mult, op1=ALU.mult),
                    reads=[("x", fc), ("F", 3), "gfT"], writes=[("F", fc % 2 + 1)])
                P.add("act", lambda e, fc=fc, tmp=tmp: e.activation(out=xT[:, fc, :], in_=tmp, func=AF.Copy, scale=rt),
                      reads=[("F", fc % 2 + 1)], writes=[("x", fc)])
                continue
            for sg, (t0, tn) in enumerate([(0, TP), (TP, 64)]):
                P.add("dve", lambda e, fc=fc, tmp=tmp, sg=sg, t0=t0, tn=tn: e.scalar_tensor_tensor(
                    out=tmp[:, t0:t0 + tn], in0=xT[:, fc, t0:t0 + tn], scalar=Acoef[sg][:, l, fc:fc + 1], in1=rstd[:, t0:t0 + tn],
                    op0=ALU.mult, op1=ALU.mult),
                    reads=[("x", fc), ("F", 3), "A1_0", "A1_1", "A2_0", "A2_1"], writes=[("F", fc % 2 + 1)])
                P.add("act", lambda e, fc=fc, tmp=tmp, sg=sg, t0=t0, tn=tn: e.activation(
                    out=hT[:, fc, t0:t0 + tn], in_=tmp[:, t0:t0 + tn], func=AF.Identity, scale=rt,
                    bias=MOD[sg][:, l, Boff + fc:Boff + fc + 1]),
                    reads=[("F", fc % 2 + 1)] + MODK, writes=[("h", fc)])

    def transposes(src_list, dstK, dkey, col0s, scale_ap=None, skeys=()):
        for src, col0, skey in zip(src_list, col0s, skeys):
            for (m, p0, np_) in [(0, 0, 8), (1, 8, 1)]:
                for p in range(p0, p0 + np_):
                    n = 128 if p < 8 else 64
                    P.add("pe", lambda e, src=src, p=p, n=n, m=m, p0=p0: e.transpose(
                        out=Mb[m][0:n, (p - p0) * 128:(p - p0) * 128 + 128], in_=src[:, 128 * p:128 * p + n], identity=identb[:]),
                        reads=[skey, "identb"], writes=[("m", m)])
                n = 128 if p0 == 0 else 64
                src_ps = Mb[m][0:n, 0:np_ * 128].rearrange("p (a b) -> p a b", b=128)
                dst = dstK[0:n, p0:p0 + np_, col0:col0 + 128]
                if scale_ap is None:
                    P.add("act", lambda e, dst=dst, src_ps=src_ps: e.activation(out=dst, in_=src_ps, func=AF.Copy),
                          reads=[("m", m)], writes=[dkey])
                else:
                    P.add("act", lambda e, dst=dst, src_ps=src_ps, n=n: e.activation(
                        out=dst, in_=src_ps, func=AF.Identity, scale=scale_ap[0:n, :]),
                        reads=[("m", m), "zinv"], writes=[dkey])

    def build_AT(k_list, q_list, kkeys, qkeys):
        nk = len(k_list)
        for g, (p0, np_) in enumerate([(0, 4), (4, 4), (8, 1)]):
            m = g % 2
            for p in range(p0, p0 + np_):
                n = 128 if p < 8 else 64
                for kc in range(nk):
                    P.add("pe", lambda e, p=p, n=n, kc=kc, m=m, p0=p0: e.matmul(
                        M[m][0:n, (p - p0) * 128:(p - p0) * 128 + n], lhsT=k_list[kc][:, 128 * p:128 * p + n],
                        rhs=q_list[kc][:, 128 * p:128 * p + n], start=(p == p0 and kc == 0), stop=(kc == nk - 1),
                        skip_group_check=True),
                        reads=list(kkeys) + list(qkeys), writes=[("m", m)])
            n = 128 if p0 < 8 else 64
            P.add("dve", lambda e, m=m, p0=p0, np_=np_, n=n: e.copy_predicated(
                out=ATb[0:n, p0:p0 + np_, 0:n], mask=maskrep[0:n, 0:np_, 0:n],
                data=M[m][0:n, 0:np_ * 128].rearrange("p (a b) -> p a b", b=128)[:, :, 0:n]),
                reads=[("m", m), "maskrep", "ATb"], writes=["ATb"])

    def sumsq_norm(ol_list, olkeys, gains, gates, gkeys, fcs, N):
        nv = len(ol_list)
        for vc in range(nv):
            P.add("act", lambda e, vc=vc: e.activation(out=Bv(vc), in_=ol_list[vc], func=AF.Square),
                  reads=[olkeys[vc]], writes=[("B", vc)])
        rstd = Fv(0)
        for tb, (t0, tn) in enumerate(TBS):
            m = tb % 2
            for vc in range(nv):
                P.add("pe", lambda e, vc=vc, t0=t0, tn=tn, m=m: e.matmul(
                    M[m][:, 0:tn], lhsT=onesb[:], rhs=Bv(vc)[:, t0:t0 + tn], start=(vc == 0), stop=(vc == nv - 1)),
                    reads=[("B", vc), "onesb"], writes=[("m", m)])
            P.add("act", lambda e, t0=t0, tn=tn, m=m: e.activation(
                out=rstd[:, t0:t0 + tn], in_=M[m][:, 0:tn], func=AF.Sqrt, bias=epst[:, (1 if N == 128 else 2):(2 if N == 128 else 3)]),
                reads=[("m", m), "epst"], writes=[("F", 0)])
        P.add("dve", lambda e: e.reciprocal(out=rstd, in_=rstd), reads=[("F", 0)], writes=[("F", 0)])
        rt = float(N) ** 0.5
        for vc in range(nv):
            P.add("dve", lambda e, vc=vc: e.scalar_tensor_tensor(
                out=ol_list[vc], in0=ol_list[vc], scalar=rt, in1=rstd, op0=ALU.mult, op1=ALU.mult),
                reads=[olkeys[vc], ("F", 0)], writes=[olkeys[vc]])
            P.add("dve", lambda e, vc=vc: e.scalar_tensor_tensor(
                out=oTg[:, fcs[vc] % 4, :], in0=ol_list[vc], scalar=gains[vc], in1=gates[vc], op0=ALU.mult, op1=ALU.mult),
                reads=[olkeys[vc], gkeys[vc], "ghnT", "grnT"], writes=[("og", fcs[vc] % 4)])

    oc = [0]

    def out_dma(dst, src, reads):
        oc[0] += 1
        P.dma("sp", lambda e: e.dma_start(out=dst, in_=src), reads=reads, writes=[("out", oc[0])], semkey="o%d" % (oc[0] % 8))

    def hgrn_load(l, hd):
        c0 = 512 * hd
        return (wload(wsrc(w_in, l, 0, 16, c0), 16), wload(wsrc(w_in, l, 0, 16, c0 + 256), 16))

    def hgrn_head(l, hd, part, slots=None):
        par = hd % 2
        F0, F1, F2, F3 = Fv(0), Fv(1), Fv(2), Fv(3)
        B0, B1, B2, B4, B5 = Bv(0), Bv(1), Bv(2), Bv(4), Bv(5)
        B3, K3 = (Bv(3), ("B", 3)) if par == 0 else (Bv(7), ("B", 7))
        B6, K6 = (Bv(6), ("B", 6)) if par == 0 else (Bv(8), ("B", 8))
        if part == "front":
            hgrn_front(l, hd, slots, F0, F1, F2, F3, B0, B1, B2, B3, K3, B4, B5, B6, K6)
        elif part == "chain":
            hgrn_chain(l, hd, par, F3, B2)
        else:
            hgrn_tail(l, hd, par, F0, F1, F3, B3, K3, B6, K6)

    def hgrn_front(l, hd, slots, F0, F1, F2, F3, B0, B1, B2, B3, K3, B4, B5, B6, K6):
        s0, s1 = slots
        w0 = lambda c: (lambda kc: wsl[:, s0, kc, c * 128:(c + 1) * 128])
        w1 = lambda c: (lambda kc: wsl[:, s1, kc, c * 128:(c + 1) * 128])
        ps = proj(w0(1), 16, hrhs, [("w", s0), "h"])
        evac_act(ps, lambda t0, tn: F0[:, t0:t0 + tn], AF.Sigmoid, [("F", 0)])
        P.add("dve", lambda e: e.tensor_scalar(out=F0, in0=F0, scalar1=omlv[:, l, hd:hd + 1], scalar2=lbv[:, l, hd:hd + 1],
                                               op0=ALU.mult, op1=ALU.add), reads=[("F", 0), "omlv", "lbv"], writes=[("F", 0)])
        P.add("act", lambda e: e.activation(out=F1, in_=F0, func=AF.Ln), reads=[("F", 0)], writes=[("F", 1)])
        P.add("dve", lambda e: e.tensor_scalar(out=B0, in0=F0, scalar1=-1.0, scalar2=1.0, op0=ALU.mult, op1=ALU.add),
              reads=[("F", 0)], writes=[("B", 0)])
        for (t0, tn) in [(0, TP), (TP, 64)]:
            P.add("dve", lambda e, t0=t0, tn=tn: e.tensor_tensor_scan(
                out=F2[:, t0:t0 + tn], data0=F1[:, t0:t0 + tn], data1=zcol[:].to_broadcast([128, tn]), initial=0.0,
                op0=ALU.add, op1=ALU.add), reads=[("F", 1), "zcol"], writes=[("F", 2)])
        F2c = c3(F2)
        P.add("dve", lambda e: e.tensor_tensor(out=c3(F1), in0=F2c, in1=F2c[:, :, 31:32].to_broadcast([128, NCH, 64]), op=ALU.subtract),
              reads=[("F", 2)], writes=[("F", 1)])
        P.add("dve", lambda e: e.tensor_copy(out=sm[:, 3, :], in_=F2c[:, :, 63]), reads=[("F", 2)], writes=["sm"])
        P.add("dve", lambda e: e.tensor_tensor(out=sm[:, 0, 1:16], in0=F2c[:, 1:16, 31], in1=F2c[:, 0:15, 31], op=ALU.subtract),
              reads=[("F", 2), "sm"], writes=["sm"])
        P.add("dve", lambda e: e.tensor_tensor(out=sm[:, 1, :], in0=F2c[:, :, 63], in1=F2c[:, :, 31], op=ALU.subtract),
              reads=[("F", 2), "sm"], writes=["sm"])
        P.add("act", lambda e: e.activation(out=dexp[:], in_=sm[:], func=AF.Exp), reads=["sm"], writes=["dexp"])
        P.add("act", lambda e: e.activation(out=F0, in_=F1, func=AF.Exp), reads=[("F", 1)], writes=[("F", 0)])
        P.add("act", lambda e: e.activation(out=F1, in_=F1, func=AF.Exp, scale=-1.0), reads=[("F", 1)], writes=[("F", 1)])
        P.add("act", lambda e: e.activation(out=F2, in_=F2, func=AF.Exp), reads=[("F", 2)], writes=[("F", 2)])
        ps = proj(w0(0), 16, hrhs, [("w", s0), "h"])
        evac_act(ps, lambda t0, tn: B1[:, t0:t0 + tn], AF.Silu, [("B", 1)])
        P.add("dve", lambda e: e.tensor_tensor(out=B2, in0=B1, in1=F0, op=ALU.mult), reads=[("B", 1), ("F", 0)], writes=[("B", 2)])
        P.add("dve", lambda e: e.tensor_tensor(out=B3, in0=B1, in1=F2, op=ALU.mult), reads=[("B", 1), ("F", 2)], writes=[K3])
        P.add("dve", lambda e: e.tensor_tensor(out=B4, in0=B0, in1=F1, op=ALU.mult), reads=[("B", 0), ("F", 1)], writes=[("B", 4)])
        ps = proj(w1(0), 16, hrhs, [("w", s1), "h"])
        evac_act(ps, lambda t0, tn: B5[:, t0:t0 + tn], AF.Copy, [("B", 5)])
        ps = proj(w1(1), 16, hrhs, [("w", s1), "h"])
        evac_act(ps, lambda t0, tn: B6[:, t0:t0 + tn], AF.Silu, [K6])
        transposes([B5], K0, "K0", [0], skeys=[("B", 5)])
        transposes([B4], K1, "K1", [0], skeys=[("B", 4)])
        build_AT([B4], [B2], [("B", 4)], [("B", 2)])

    def hgrn_chain(l, hd, par, F3, B2):
        U = [Sf[:, 0:128], Sf[:, 128:256]]
        UK = ["U0", "U1"]
        for p in range(NPAIR):
            n = 128 if p < 8 else 64
            q4 = p % 4
            P.add("pe", lambda e, p=p, n=n, q4=q4: e.matmul(
                M[1][:, q4 * 128:q4 * 128 + n], lhsT=K0[0:n, p, 0:128], rhs=ATb[0:n, p, 0:n], start=(q4 == 0), stop=False,
                skip_group_check=True), reads=["K0", "ATb"], writes=[("m", 1)])
            if p % 2 == 0:
                for cidx, c in enumerate(range(2 * p, min(2 * p + 4, NCH))):
                    pp, half = c // 2, c % 2
                    pbank = M[0] if half == 0 else PJ[0][0]
                    pkey = ("m", 0) if half == 0 else ("pj", 0, 0)
                    P.add("pe", lambda e, cidx=cidx, pp=pp, half=half, pbank=pbank: e.matmul(
                        pbank[:, (cidx // 2) * 128:(cidx // 2 + 1) * 128], lhsT=K1[64 * half:64 * half + 64, pp, 0:128],
                        rhs=K0[64 * half:64 * half + 64, pp, 0:128], start=(cidx < 2), stop=True, skip_group_check=True),
                        reads=["K0", "K1"], writes=[pkey])
            for c in range(2 * p, min(2 * p + 2, NCH)):
                half = c % 2
                first = (c == 0 or c == 16)
                cidx = c % 4
                pbank = M[0] if half == 0 else PJ[0][0]
                pkey = ("m", 0) if half == 0 else ("pj", 0, 0)
                psl = pbank[:, (cidx // 2) * 128:(cidx // 2 + 1) * 128]
                cur, prv = c % 2, (c + 1) % 2
                if first:
                    P.add("dve", lambda e, psl=psl, cur=cur: e.tensor_copy(out=U[cur], in_=psl), reads=[pkey], writes=[UK[cur]])
                else:
                    P.add("act", lambda e, c=c, prv=prv: e.activation(out=Sb[:, 0:128], in_=U[prv], func=AF.Identity,
                                                                      scale=dexp[:, 0, c:c + 1]),
                          reads=[UK[prv], "dexp"], writes=["Sb"])
                    P.add("pe", lambda e, c=c, q4=q4, half=half: e.matmul(
                        M[1][:, q4 * 128 + 64 * half:q4 * 128 + 64 * half + 64], lhsT=Sb[:, 0:128], rhs=B2[:, 64 * c:64 * c + 64],
                        start=False, stop=True, skip_group_check=True), reads=["Sb", ("B", 2)], writes=[("m", 1)])
                    P.add("dve", lambda e, c=c, psl=psl, cur=cur, prv=prv: e.scalar_tensor_tensor(
                        out=U[cur], in0=U[prv], scalar=dexp[:, 0, c:c + 1], in1=psl, op0=ALU.mult, op1=ALU.add),
                        reads=[UK[prv], pkey, "dexp"], writes=[UK[cur]])
                if c == 15:
                    P.add("dve", lambda e, cur=cur: e.tensor_scalar(out=GI[:, 0:128], in0=U[cur], scalar1=dexp[:, 1, 15:16], scalar2=None,
                                                                    op0=ALU.mult), reads=[UK[cur], "dexp"], writes=["GI"])
                if c == 16:
                    P.add("dve", lambda e, cur=cur: e.tensor_scalar(out=SLs[:, 0:128], in0=U[cur], scalar1=dexp[:, 1, 16:17], scalar2=None,
                                                                    op0=ALU.mult), reads=[UK[cur], "dexp"], writes=["SLs"])
            if q4 == 3 or p == 8:
                g = p // 4
                ncol = 512 if p < 8 else 64
                P.add("act", lambda e, g=g, ncol=ncol: e.activation(out=F3[:, 512 * g:512 * g + ncol], in_=M[1][:, 0:ncol], func=AF.Copy),
                      reads=[("m", 1)], writes=[("F", 3)])
        P.add("act", lambda e: e.activation(out=GI[:, 128:129], in_=dexp[:, 3, 15:16], func=AF.Copy), reads=["dexp"], writes=["GI"])
        P.add("act", lambda e: e.activation(out=dts[:, par, :], in_=dexp[:, 3, 15:17], func=AF.Copy), reads=["dexp"], writes=[("dts", par)])
        P.dma("sp", lambda e: e.dma_start(out=agA_in[l][hd].ap(), in_=GI[:]), reads=["GI"], writes=[("agin", l, hd)], semkey="ag_st")
        P.dma("pool", lambda e: e.collective_compute("AllGather", ALU.bypass, replica_groups=[list(range(NCORE))],
                                                      ins=[agA_in[l][hd].ap()], outs=[agA_out[l][hd].ap()]),
              reads=[("agin", l, hd)], writes=[("agout", l, hd)], semkey="cc", inc=1)

    def hgrn_tail(l, hd, par, F0, F1, F3, B3, K3, B6, K6):
        P.dma("sp", lambda e: e.dma_start(out=gat[:], in_=agA_out[l][hd].ap().rearrange("(r p) f -> p r f", p=128)),
              reads=[("agout", l, hd)], writes=["gatA", "gatB"], semkey="ag_ld")
        P.add("dve", lambda e: e.scalar_tensor_tensor(out=acoef[:], in0=gat[:, :, 128], scalar=-1.0, in1=maskA[:],
                                                      op0=ALU.add, op1=ALU.mult), reads=["gatA", "gatB", "maskA"], writes=["acoef"])
        P.add("dve", lambda e: e.tensor_scalar(out=acoef[:], in0=acoef[:], scalar1=1.0, scalar2=None, op0=ALU.add),
              reads=["acoef"], writes=["acoef"])
        P.add("dve", lambda e: e.tensor_tensor(out=gat[:, :, 0:128], in0=gat[:, :, 0:128],
                                               in1=maskA[:].unsqueeze(2).to_broadcast([128, 8, 128]), op=ALU.mult),
              reads=["gatA", "gatB", "maskA"], writes=["gatA", "gatB"])
        Sin = F1[:, 0:128]
        P.add("dve", lambda e: e.tensor_copy(out=Sin, in_=gat[:, 0, 0:128]), reads=["gatA", "gatB"], writes=[("F", 1)])
        for j in range(1, 7):
            P.add("dve", lambda e, j=j: e.scalar_tensor_tensor(out=Sin, in0=Sin, scalar=acoef[:, j:j + 1], in1=gat[:, j, 0:128],
                                                               op0=ALU.mult, op1=ALU.add),
                  reads=["gatA", "gatB", "acoef", ("F", 1)], writes=[("F", 1)])
        P.add("act", lambda e: e.activation(out=Sinb[:, 0:128], in_=Sin, func=AF.Copy), reads=[("F", 1)], writes=["Sinb"])
        P.dma("sp", lambda e: e.dma_start(out=stS[:, 0:128], in_=sta_d[l, hd]), writes=["gatB"], semkey="stS")
        P.add("act", lambda e: e.activation(out=Sinsb[:, 0:128], in_=stS[:, 0:128], func=AF.Copy), reads=["gatB"], writes=["Sinsb"])
        P.add("dve", lambda e: e.scalar_tensor_tensor(out=SO[:, 0:128], in0=Sin, scalar=dts[:, par, 0:1], in1=GI[:, 0:128],
                                                      op0=ALU.mult, op1=ALU.add), reads=[("F", 1), ("dts", par), "GI"], writes=["gatA"])
        P.add("dve", lambda e: e.scalar_tensor_tensor(out=SO[:, 128:256], in0=stS[:, 0:128], scalar=dts[:, par, 1:2], in1=SLs[:, 0:128],
                                                      op0=ALU.mult, op1=ALU.add), reads=["gatB", ("dts", par), "SLs"], writes=["gatA"])
        oc[0] += 1
        P.dma("sp", lambda e: e.dma_start(out=soa_d[l, 0, hd], in_=SO[:, 0:128]), reads=["gatA"], writes=[("out", oc[0]), "gatA"], semkey="so")
        oc[0] += 1
        P.dma("sp", lambda e: e.dma_start(out=soa_d[l, 1, hd], in_=SO[:, 128:256]), reads=["gatA"], writes=[("out", oc[0]), "gatA"], semkey="so1")
        for tb, (t0, tn) in enumerate(TBS):
            m = tb % 2
            lhs = Sinb if tb < 2 else Sinsb
            P.add("pe", lambda e, t0=t0, tn=tn, m=m, lhs=lhs: e.matmul(M[m][:, 0:tn], lhsT=lhs[:, 0:128], rhs=B3[:, t0:t0 + tn],
                                                                     start=True, stop=True),
                  reads=["Sinb", "Sinsb", K3], writes=[("m", m)])
            P.add("dve", lambda e, t0=t0, tn=tn, m=m: e.tensor_tensor(out=F3[:, t0:t0 + tn], in0=M[m][:, 0:tn], in1=F3[:, t0:t0 + tn], op=ALU.add),
                  reads=[("m", m), ("F", 3)], writes=[("F", 3)])
        sumsq_norm([F3], [("F", 3)], [ghnT[:, l, hd:hd + 1]], [B6], [K6], [hd], 128)

    def ret_head(l, hb, mid_hook=None):
        c0 = 4096 + 1024 * hb
        F0, F1, F2, F3 = Fv(0), Fv(1), Fv(2), Fv(3)
        B = [Bv(k) for k in range(8)]
        g64 = GAM[hb] ** 64

        def rope(blk, o1, o2, k1, k2):
            s = wload(wsrc(w_in, l, 0, 16, c0 + 256 * blk), 16)
            for cc, dst, dk in [(0, B[0], ("B", 0)), (1, B[1], ("B", 1))]:
                ps = proj(lambda kc, cc=cc: wsl[:, s, kc, cc * 128:(cc + 1) * 128], 16, hrhs, [("w", s), "h"])
                evac_act(ps, lambda t0, tn, dst=dst: dst[:, t0:t0 + tn], AF.Copy, [dk])
            P.add("dve", lambda e: e.tensor_tensor(out=F0, in0=B[0], in1=cosT[:], op=ALU.mult), reads=[("B", 0), "cos"], writes=[("F", 0)])
            P.add("dve", lambda e: e.tensor_tensor(out=F1, in0=B[1], in1=sinT[:], op=ALU.mult), reads=[("B", 1), "sin"], writes=[("F", 1)])
            P.add("dve", lambda e: e.tensor_tensor(out=o1, in0=F0, in1=F1, op=ALU.subtract), reads=[("F", 0), ("F", 1)], writes=[k1])
            P.add("dve", lambda e: e.tensor_tensor(out=F0, in0=B[1], in1=cosT[:], op=ALU.mult), reads=[("B", 1), "cos"], writes=[("F", 0)])
            P.add("dve", lambda e: e.tensor_tensor(out=F1, in0=B[0], in1=sinT[:], op=ALU.mult), reads=[("B", 0), "sin"], writes=[("F", 1)])
            P.add("dve", lambda e: e.tensor_tensor(out=o2, in0=F0, in1=F1, op=ALU.add), reads=[("F", 0), ("F", 1)], writes=[k2])

        rope(0, B[2], B[3], ("B", 2), ("B", 3))
        rope(1, B[4], B[5], ("B", 4), ("B", 5))
        QR = [B[2], B[3]]
        KR = [B[4], B[5]]
        sv = wload(wsrc(w_in, l, 0, 16, c0 + 512), 16)
        for cc in range(2):
            ps = proj(lambda kc, cc=cc: wsl[:, sv, kc, cc * 128:(cc + 1) * 128], 16, hrhs, [("w", sv), "h"])
            evac_act(ps, lambda t0, tn, cc=cc: B[cc][:, t0:t0 + tn], AF.Copy, [("B", cc)])
        transposes([B[0], B[1]], K0, "K0", [0, 128], scale_ap=zinv[:, hb:hb + 1], skeys=[("B", 0), ("B", 1)])
        transposes(KR, K1, "K1", [0, 128], skeys=[("B", 4), ("B", 5)])
        build_AT(KR, QR, [("B", 4), ("B", 5)], [("B", 2), ("B", 3)])
        OL = [F2, F3]
        UB = [Sf, SLs]
        UK = ["Sf", "SLs"]
        for p in range(NPAIR):
            n = 128 if p < 8 else 64
            q2 = p % 2
            for vc in range(2):
                P.add("pe", lambda e, p=p, n=n, q2=q2, vc=vc: e.matmul(
                    M[1][:, vc * 256 + q2 * 128:vc * 256 + q2 * 128 + n], lhsT=K0[0:n, p, vc * 128:(vc + 1) * 128], rhs=ATb[0:n, p, 0:n],
                    start=(q2 == 0 and vc == 0), stop=False, skip_group_check=True), reads=["K0", "ATb"], writes=[("m", 1)])
            for c in range(2 * p, min(2 * p + 2, NCH)):
                half = c % 2
                first = (c == 0 or c == 16)
                cur, prv = c % 2, (c + 1) % 2
                pbank = M[0] if half == 0 else PJ[0][0]
                pkey = ("m", 0) if half == 0 else ("pj", 0, 0)
                for kc in range(2):
                    P.add("pe", lambda e, p=p, half=half, kc=kc, pbank=pbank: e.matmul(
                        pbank[:, kc * 256:(kc + 1) * 256], lhsT=K1[64 * half:64 * half + 64, p, kc * 128:(kc + 1) * 128],
                        rhs=K0[64 * half:64 * half + 64, p, 0:256], start=(kc == 0), stop=True, skip_group_check=True),
                        reads=["K0", "K1"], writes=[pkey])
                if first:
                    P.add("dve", lambda e, cur=cur, pbank=pbank: e.tensor_copy(out=UB[cur][:], in_=pbank[:, :]), reads=[pkey], writes=[UK[cur]])
                else:
                    P.add("act", lambda e, prv=prv: e.activation(out=Sb[:], in_=UB[prv][:], func=AF.Copy, scale=g64),
                          reads=[UK[prv]], writes=["Sb"])
                    for vc in range(2):
                        for kc in range(2):
                            P.add("pe", lambda e, c=c, q2=q2, half=half, vc=vc, kc=kc: e.matmul(
                                M[1][:, vc * 256 + q2 * 128 + 64 * half:vc * 256 + q2 * 128 + 64 * half + 64],
                                lhsT=Sb[:, kc * 256 + vc * 128:kc * 256 + vc * 128 + 128], rhs=QR[kc][:, 64 * c:64 * c + 64],
                                start=False, stop=(kc == 1), skip_group_check=True),
                                reads=["Sb", ("B", 2), ("B", 3)], writes=[("m", 1)])
                    P.add("dve", lambda e, cur=cur, prv=prv, pbank=pbank: e.scalar_tensor_tensor(
                        out=UB[cur][:], in0=UB[prv][:], scalar=g64, in1=pbank[:, :], op0=ALU.mult, op1=ALU.add),
                        reads=[UK[prv], pkey], writes=[UK[cur]])
                if c == 15:
                    P.add("dve", lambda e, cur=cur: e.tensor_scalar(out=F0[:, 0:512], in0=UB[cur][:], scalar1=g64, scalar2=None, op0=ALU.mult),
                          reads=[UK[cur]], writes=[("F", 0)])
                if c == 16:
                    P.add("dve", lambda e: e.tensor_scalar(out=SLs[:], in0=Sf[:], scalar1=g64, scalar2=None, op0=ALU.mult),
                          reads=["Sf"], writes=["SLs"])
            if q2 == 1 or p == 8:
                g = p // 2
                nch = 4 if p < 8 else 1
                for vc in range(2):
                    src = M[1][:, vc * 256:vc * 256 + 256].rearrange("p (c s) -> p c s", s=64)[:, 0:nch, :]
                    P.add("dve", lambda e, vc=vc, g=g, nch=nch, src=src: e.tensor_tensor(
                        out=c3(OL[vc])[:, 4 * g:4 * g + nch, :], in0=src, in1=xi4[:, hb:hb + 1, :].to_broadcast([128, nch, 64]), op=ALU.mult),
                        reads=[("m", 1), "xi4"], writes=[("F", 2 + vc)])
        oc[0] += 0
        P.dma("sp", lambda e: e.dma_start(out=agB_in[l][hb].ap(), in_=F0[:, 0:512]), reads=[("F", 0)], writes=[("aginB", l, hb)], semkey="ag_st")
        P.dma("pool", lambda e: e.collective_compute("AllGather", ALU.bypass, replica_groups=[list(range(NCORE))],
                                                      ins=[agB_in[l][hb].ap()], outs=[agB_out[l][hb].ap()]),
              reads=[("aginB", l, hb)], writes=[("agoutB", l, hb)], semkey="cc", inc=1)
        if mid_hook is not None:
            mid_hook()
        sg_ = wload(wsrc(w_in, l, 0, 16, c0 + 768), 16)
        for cc in range(2):
            ps = proj(lambda kc, cc=cc: wsl[:, sg_, kc, cc * 128:(cc + 1) * 128], 16, hrhs, [("w", sg_), "h"])
            evac_act(ps, lambda t0, tn, cc=cc: B[6 + cc][:, t0:t0 + tn], AF.Silu, [("B", 6 + cc)])
        Sin = F1[:, 0:512]
        for q in range(4):
            P.dma("sp", lambda e, q=q: e.dma_start(out=gat[:, :, 0:128],
                                                   in_=agB_out[l][hb].ap()[:, q * 128:(q + 1) * 128].rearrange("(r p) f -> p r f", p=128)),
                  reads=[("agoutB", l, hb)], writes=["gatA", "gatB"], semkey="ag_ld")
            P.add("dve", lambda e, q=q: e.tensor_scalar(out=Sin[:, q * 128:(q + 1) * 128], in0=gat[:, 0, 0:128], scalar1=coefB[:, hb, 0:1],
                                                        scalar2=None, op0=ALU.mult), reads=["gatA", "gatB", "coefB"], writes=[("F", 1)])
            for j in range(1, 7):
                P.add("dve", lambda e, q=q, j=j: e.scalar_tensor_tensor(
                    out=Sin[:, q * 128:(q + 1) * 128], in0=gat[:, j, 0:128], scalar=coefB[:, hb, j:j + 1], in1=Sin[:, q * 128:(q + 1) * 128],
                    op0=ALU.mult, op1=ALU.add), reads=["gatA", "gatB", "coefB", ("F", 1)], writes=[("F", 1)])
        P.add("act", lambda e: e.activation(out=Sinb[:], in_=Sin, func=AF.Copy), reads=[("F", 1)], writes=["Sinb"])
        P.dma("sp", lambda e: e.dma_start(out=stS[:].rearrange("p (kc v) -> p kc v", kc=2),
                                          in_=stb_d[l, hb].rearrange("(kc p) v -> p kc v", p=128)), writes=["gatB"], semkey="stS")
        P.add("act", lambda e: e.activation(out=Sinsb[:], in_=stS[:], func=AF.Copy), reads=["gatB"], writes=["Sinsb"])
        P.add("dve", lambda e: e.scalar_tensor_tensor(out=SO[:], in0=Sin, scalar=GAM[hb] ** 1024, in1=F0[:, 0:512], op0=ALU.mult, op1=ALU.add),
              reads=[("F", 1), ("F", 0)], writes=["gatA"])
        oc[0] += 1
        P.dma("sp", lambda e: e.dma_start(out=sob_d[l, 0, hb].rearrange("(kc p) v -> p kc v", p=128),
                                          in_=SO[:].rearrange("p (kc v) -> p kc v", kc=2)),
              reads=["gatA"], writes=[("out", oc[0]), "gatA"], semkey="so")
        P.add("dve", lambda e: e.scalar_tensor_tensor(out=SO[:], in0=stS[:], scalar=g64, in1=SLs[:], op0=ALU.mult, op1=ALU.add),
              reads=["gatB", "SLs"], writes=["gatA"])
        oc[0] += 1
        P.dma("sp", lambda e: e.dma_start(out=sob_d[l, 1, hb].rearrange("(kc p) v -> p kc v", p=128),
                                          in_=SO[:].rearrange("p (kc v) -> p kc v", kc=2)),
              reads=["gatA"], writes=[("out", oc[0]), "gatA"], semkey="so")
        GX = F1
        GX = F0
        P.add("dve", lambda e: e.tensor_tensor(out=c3(GX), in0=xi4[:, hb:hb + 1, :].to_broadcast([128, NCH, 64]),
                                               in1=gc4[:, hb, :].unsqueeze(2).to_broadcast([128, NCH, 64]), op=ALU.mult),
              reads=["xi4", "gc4"], writes=[("F", 0)])
        i = 0
        for vc in range(2):
            for tb, (t0, tn) in enumerate(TBS):
                m = i % 2
                i += 1
                lhs = Sinb if tb < 2 else Sinsb
                for kc in range(2):
                    P.add("pe", lambda e, t0=t0, tn=tn, m=m, lhs=lhs, kc=kc, vc=vc: e.matmul(
                        M[m][:, 0:tn], lhsT=lhs[:, kc * 256 + vc * 128:kc * 256 + vc * 128 + 128], rhs=QR[kc][:, t0:t0 + tn],
                        start=(kc == 0), stop=(kc == 1)), reads=["Sinb", "Sinsb", ("B", 2), ("B", 3)], writes=[("m", m)])
                tmp = Bv(0)[:, 0:1024].bitcast(F32)
                P.add("dve", lambda e, t0=t0, tn=tn, m=m, tmp=tmp: e.tensor_tensor(out=tmp[:, 0:tn], in0=M[m][:, 0:tn], in1=GX[:, t0:t0 + tn], op=ALU.mult),
                      reads=[("m", m), ("F", 0)], writes=[("B", 0)])
                P.add("dve", lambda e, t0=t0, tn=tn, vc=vc, tmp=tmp: e.tensor_tensor(out=OL[vc][:, t0:t0 + tn], in0=OL[vc][:, t0:t0 + tn], in1=tmp[:, 0:tn], op=ALU.add),
                      reads=[("B", 0), ("F", 2 + vc)], writes=[("F", 2 + vc)])
        sumsq_norm(OL, [("F", 2), ("F", 3)], [grnT[:, l, 2 * hb:2 * hb + 1], grnT[:, l, 2 * hb + 1:2 * hb + 2]],
                   [B[6], B[7]], [("B", 6), ("B", 7)], [2 * hb, 2 * hb + 1], 256)

    def wout_group(l, gi):
        for cb in range(8):
            s = wload(wsrc(w_out, l, 512 * gi, 4, 256 * cb), 4)
            for cc in range(2):
                fc = 2 * cb + cc
                ps = proj(lambda kc, cc=cc, s=s: wsl[:, s, kc, cc * 128:(cc + 1) * 128], 4,
                          lambda kc, t0, tn: oTg[:, kc, t0:t0 + tn], [("w", s)] + [("og", k) for k in range(4)])
                for tb, (t0, tn) in enumerate(TBS):
                    sg = 0 if tb < 2 else 1
                    P.add("dve", lambda e, fc=fc, tb=tb, t0=t0, tn=tn, sg=sg, ps=ps: e.scalar_tensor_tensor(
                        out=xT[:, fc, t0:t0 + tn], in0=PJ[ps][tb][:, 0:tn], scalar=MOD[sg][:, l, 32 + fc:33 + fc], in1=xT[:, fc, t0:t0 + tn],
                        op0=ALU.mult, op1=ALU.add), reads=[("pj", ps, tb), ("x", fc)] + MODK, writes=[("x", fc)])

    def ffn(l):
        for j in range(4):
            for blk in range(8):
                s = wload(wsrc(w_up, l, 0, 16, 2048 * j + 256 * blk), 16)
                for cc in range(2):
                    ufc = 2 * blk + cc
                    ps = proj(lambda kc, cc=cc, s=s: wsl[:, s, kc, cc * 128:(cc + 1) * 128], 16, hrhs, [("w", s), "h"])
                    tmpk, tkey = (K0f, "K0") if ufc % 2 == 0 else (K1f, "K1")
                    evac_act(ps, lambda t0, tn, tmpk=tmpk: tmpk[:, t0:t0 + tn], AF.Relu, [tkey])
                    P.add("dve", lambda e, ufc=ufc, tmpk=tmpk: e.tensor_tensor(out=uT[:, ufc, :], in0=tmpk, in1=tmpk, op=ALU.mult),
                          reads=[tkey], writes=[ukey(ufc)])
            ukeys = sorted(set(ukey(f) for f in range(16)))
            for cb in range(8):
                s = wload(wsrc(w_down, l, 2048 * j, 16, 256 * cb), 16)
                for cc in range(2):
                    fc = 2 * cb + cc
                    ps = proj(lambda kc, cc=cc, s=s: wsl[:, s, kc, cc * 128:(cc + 1) * 128], 16,
                              lambda kc, t0, tn: uT[:, kc, t0:t0 + tn], [("w", s)] + ukeys)
                    for tb, (t0, tn) in enumerate(TBS):
                        sg = 0 if tb < 2 else 1
                        P.add("dve", lambda e, fc=fc, tb=tb, t0=t0, tn=tn, sg=sg, ps=ps: e.scalar_tensor_tensor(
                            out=xT[:, fc, t0:t0 + tn], in0=PJ[ps][tb][:, 0:tn], scalar=MOD[sg][:, l, 80 + fc:81 + fc], in1=xT[:, fc, t0:t0 + tn],
                            op0=ALU.mult, op1=ALU.add), reads=[("pj", ps, tb), ("x", fc)] + MODK, writes=[("x", fc)])

    def finish():
        P.add("sp", None, reads=[("out", k) for k in range(1, oc[0] + 1)])
        for eng_ in ("pe", "act", "dve", "pool"):
            pass
        P.emit(nc, st)
        st.close()
        return nc

    for l in range(L if DBG_STOP != 9 else 0):
        if DBG_STOP == 2:
            return finish()
        norm(l, A1, 0)
        if DBG_STOP == 3:
            return finish()
        slots = hgrn_load(l, 0)
        for hd in range(8):
            hgrn_head(l, hd, "front", slots)
            if hd >= 1:
                hgrn_head(l, hd - 1, "tail")
                if (hd - 1) % 4 == 3:
                    wout_group(l, (hd - 1) // 4)
            if hd < 7:
                slots = hgrn_load(l, hd + 1)
            hgrn_head(l, hd, "chain")
        hgrn_head(l, 7, "tail")
        if DBG_STOP == 5:
            return finish()
        for hb in range(4):
            hook = None
            if hb == 0:
                hook = lambda l=l: wout_group(l, 1)
            if hb == 2:
                hook = lambda l=l: wout_group(l, 2)
            ret_head(l, hb, hook)
        wout_group(l, 3)
        if DBG_STOP == 7:
            return finish()
        norm(l, A2, 48)
        ffn(l)
        if DBG_STOP == 8:
            return finish()

    import os
    SUB = int(os.environ.get('DBG_SUB', '0'))
    if SUB != 3:
        norm(0, None, 0, final=True)
    for ti in range(NPAIR if SUB in (0, 3) else (0 if SUB == 1 else 8)):
        tok0 = 128 * ti if ti < 8 else T - 128
        r0 = 0 if ti < 8 else 64
        for g in range(4):
            s = next_set()
            for q in range(4):
                fc = 4 * g + q
                P.add("pe", lambda e, s=s, q=q, fc=fc, tok0=tok0: e.transpose(
                    out=PJ[s][0][:, q * 128:(q + 1) * 128], in_=xT[:, fc, tok0:tok0 + 128], identity=identf[:]),
                    reads=[("x", fc), "identf"], writes=[("pj", s, 0)])
            P.add("act", lambda e, s=s, g=g, r0=r0: e.activation(out=stage[r0:128, 512 * g:512 * g + 512], in_=PJ[s][0][r0:128, :], func=AF.Copy),
                  reads=[("pj", s, 0)], writes=[("F", 0), ("F", 1)])
        oc[0] += 1
        P.dma("sp", lambda e, tok0=tok0, r0=r0: e.dma_start(out=y_d[tok0 + r0:tok0 + 128, :], in_=stage[r0:128, :]),
              reads=[("F", 0), ("F", 1)], writes=[("out", oc[0]), ("F", 0), ("F", 1)], semkey="yout")
    P.add("sp", None, reads=[("out", k) for k in range(1, oc[0] + 1)])
    P.emit(nc, st)
    st.close()
    return nc


_NC = None


def _host_tables(i):
    half = 128
    inv_freq = (1.0 / (10000.0 ** np.linspace(0.0, 1.0, half, dtype=np.float32))).astype(np.float32)
    pos = np.concatenate([1024 * i + np.arange(TP), 2048 + np.arange(64)]).astype(np.float32)
    ang = (pos[:, None] * inv_freq[None, :]).astype(np.float32)
    cosT = np.ascontiguousarray(np.cos(ang).astype(np.float32).T).astype(ml_dtypes.bfloat16)
    sinT = np.ascontiguousarray(np.sin(ang).astype(np.float32).T).astype(ml_dtypes.bfloat16)
    m = np.zeros((128, 128), np.uint8)
    for b in range(2):
        m[64 * b:64 * b + 64, 64 * b:64 * b + 64] = np.triu(np.ones((64, 64), np.uint8))
    maskrep = np.ascontiguousarray(np.broadcast_to(m[:, None, :], (128, 4, 128)))
    gam = np.array(GAM, np.float64)
    s = np.arange(64)
    xi4 = np.broadcast_to((gam[:, None] ** (s[None, :] + 1)).astype(np.float32)[None], (128, 4, 64)).copy()
    gc = np.ones((4, NCH), np.float64)
    gc[:, :16] = gam[:, None] ** (64.0 * np.arange(16)[None, :])
    gc4 = np.broadcast_to(gc.astype(np.float32)[None], (128, 4, NCH)).copy()
    pp = np.arange(128) % 64
    zinv = ((gam[None, :] ** (-(pp[:, None] + 1.0))) / 16.0).astype(np.float32)
    maskA = np.broadcast_to((np.arange(8) < i).astype(np.float32)[None], (128, 8)).copy()
    cb = np.zeros((4, 8), np.float64)
    for j in range(8):
        if j < i:
            cb[:, j] = gam ** (1024.0 * (i - 1 - j))
    coefB = np.broadcast_to(cb.astype(np.float32)[None], (128, 4, 8)).copy()
    oh = np.zeros((128, 9), np.float32)
    oh[:, 1 + i] = 1.0
    return dict(cosT=cosT, sinT=sinT, maskrep=maskrep, xi4=xi4, gc4=gc4, zinv=zinv, maskA=maskA, coefB=coefB, onehot=oh)


def kernel(x_prompt, x_sample, state_hgrn, state_ret, c_prompt, c_sample, lb_logits, w_ada, b_ada,
           norm1_g, norm2_g, w_in, hgrn_norm_g, ret_norm_g, w_out, w_up, w_down, final_g):
    global _NC
    f = lambda a: np.ascontiguousarray(np.asarray(a, dtype=np.float32))
    x_prompt, x_sample, state_hgrn, state_ret = f(x_prompt), f(x_sample), f(state_hgrn), f(state_ret)
    w_ada, w_in, w_out, w_up, w_down = f(w_ada), f(w_in), f(w_out), f(w_up), f(w_down)
    idx = []
    for hd in range(8):
        for part in range(4):
            idx.append(np.arange(1024 * part + 128 * hd, 1024 * part + 128 * hd + 128))
    for hb in range(4):
        for part in range(4):
            idx.append(np.arange(4096 + 1024 * part + 256 * hb, 4096 + 1024 * part + 256 * hb + 256))
    idx = np.concatenate(idx)
    w_in_p = np.ascontiguousarray(w_in[:, :, idx])
    c_all = np.concatenate([f(c_prompt), f(c_sample)], 0)
    cT = np.ascontiguousarray(c_all.reshape(9, 16, 128).transpose(2, 1, 0))
    vT = lambda v, n: np.ascontiguousarray(f(v).reshape(L, n, 128).transpose(2, 0, 1))
    g1T, g2T = vT(norm1_g, 16), vT(norm2_g, 16)
    gfT = np.ascontiguousarray(f(final_g).reshape(16, 128).T)
    ghnT, grnT = vT(hgrn_norm_g, 8), vT(ret_norm_g, 8)
    lblT = np.ascontiguousarray(f(lb_logits).reshape(L, 8, 128).transpose(2, 1, 0))
    identb = np.eye(128).astype(ml_dtypes.bfloat16)
    identf = np.eye(128).astype(np.float32)
    b_ada = f(b_ada)
    if _NC is None or _NC[0] != L:
        _NC = (L, build_program())
    in_maps = []
    tiny = np.zeros((1, 128, 256), np.float32)
    for i in range(NCORE):
        d = dict(
            xin=np.ascontiguousarray(np.concatenate([x_prompt[0, 1024 * i:1024 * (i + 1)], x_sample[i]], 0)),
            w_in=w_in_p if _need("w_in") else tiny, w_out=w_out if _need("w_out") else tiny,
            w_up=w_up if _need("w_up") else tiny, w_down=w_down if _need("w_down") else tiny,
            w_ada=np.ascontiguousarray(w_ada[:, :, 1536 * i:1536 * (i + 1)]),
            cT=cT, bT=np.ascontiguousarray(b_ada[:, 1536 * i:1536 * (i + 1)].reshape(L, 12, 128).transpose(2, 0, 1)),
            g1T=g1T, g2T=g2T, gfT=gfT, ghnT=ghnT, grnT=grnT, lblT=lblT, identb=identb, identf=identf,
            st_a=np.ascontiguousarray(state_hgrn[:, i]), st_b=np.ascontiguousarray(state_ret[:, i]),
        )
        d.update(_host_tables(i))
        in_maps.append(d)
    res = run_bass_kernel_spmd(_NC[1], in_maps, core_ids=list(range(NCORE)))
    r = res.results
    y_prompt = np.concatenate([r[i]["y"][:TP] for i in range(NCORE)], 0)[None]
    y_sample = np.stack([r[i]["y"][TP:] for i in range(NCORE)], 0)
    sa_p = r[NCORE - 1]["so_a"][:, 0][:, None]
    sb_p = r[NCORE - 1]["so_b"][:, 0][:, None]
    sa_s = np.stack([r[i]["so_a"][:, 1] for i in range(NCORE)], 1)
    sb_s = np.stack([r[i]["so_b"][:, 1] for i in range(NCORE)], 1)
    return (y_prompt.astype(np.float32), y_sample.astype(np.float32), sa_p.astype(np.float32), sb_p.astype(np.float32),
            sa_s.astype(np.float32), sb_s.astype(np.float32))
```
